# Optimizing a Trainium2 kernel written in Bass

```python
import jax, jax.numpy as jnp
from jax import lax
import numpy as np

D_MODEL = 1024
BATCH = 8
SEQ = 8192
DEPTH = 4
DEC_BATCH = 16
DEC_SEQ = 32
PAST_LEN = 2048

CHUNK = 64
N_MIXERS = 2
N_SGU_LAYERS = (DEPTH + 1) // 2
N_POOL_LAYERS = DEPTH // 2
D_SGU = 2 * D_MODEL
SGU_GROUPS = 8
SGU_GROUP_DIM = D_SGU // SGU_GROUPS
SGU_LEN = 128
POOL_WINDOWS = (2, 4, 8, 16)
POOL_GROUPS = len(POOL_WINDOWS)
POOL_GROUP_DIM = D_MODEL // POOL_GROUPS
POOL_MAX_W = max(POOL_WINDOWS)
POOL_STATE = POOL_MAX_W - 1
D_FF = 4 * D_MODEL
N_MOD = 6
EPS = 1e-6

kernel_name = "hybrid_sgu_pool_stream_encoder_step"


def rmsnorm(x, g):
    xf = x.astype(jnp.float32)
    y = xf * lax.rsqrt(jnp.mean(xf * xf, axis=-1, keepdims=True) + EPS)
    return (y * g.astype(jnp.float32)).astype(x.dtype)


def sgu_chunk_mask():
    pos = jnp.arange(SGU_LEN)
    return (pos[None, :] // CHUNK) <= (pos[:, None] // CHUNK)


def sgu_mixer(h, w_in, g_v, w_s, b_s, w_out, T):
    bsz, L, _ = h.shape
    z = jax.nn.gelu(h @ w_in)
    u, v = jnp.split(z, 2, axis=-1)
    v = rmsnorm(v, g_v)
    vr = v.reshape(bsz, L // T, T, SGU_GROUPS, SGU_GROUP_DIM)
    w = w_s[:, :T, :T] * sgu_chunk_mask()[:T, :T].astype(w_s.dtype)
    s = jnp.einsum('gij,bnjgd->bnigd', w, vr)
    s = s + jnp.transpose(b_s[:, :T])[None, None, :, :, None]
    s = s.reshape(bsz, L, D_SGU)
    return (u * s) @ w_out, v


def pool_mixer(h_ext, n_out, w_grp, scale):
    bsz, L, _ = h_ext.shape
    hf = h_ext.astype(jnp.float32)
    csum = jnp.pad(jnp.cumsum(hf, axis=1), ((0, 0), (POOL_MAX_W, 0), (0, 0)))
    rows = jnp.arange(L - n_out, L)
    start = POOL_MAX_W + L - n_out
    outs = []
    for g, w in enumerate(POOL_WINDOWS):
        sl = slice(g * POOL_GROUP_DIM, (g + 1) * POOL_GROUP_DIM)
        win = csum[:, start:start + n_out, sl] - csum[:, start - w:start - w + n_out, sl]
        cnt = jnp.minimum(w, rows + 1).astype(jnp.float32)[None, :, None]
        outs.append(win / cnt)
    pooled = jnp.concatenate(outs, axis=-1) - hf[:, L - n_out:]
    pooled = pooled.astype(h_ext.dtype).reshape(bsz, n_out, POOL_GROUPS, POOL_GROUP_DIM)
    y = jnp.einsum('btgc,gcd->btgd', pooled, w_grp).reshape(bsz, n_out, D_MODEL)
    return y * scale


def trunk(x, c, pool_cache, g_norm, w_ada, b_ada, sgu_w_in, sgu_g_v, sgu_w_s, sgu_b_s,
          sgu_w_out, pool_w_grp, pool_scale, ffn_w_up, ffn_w_down):
    sample = pool_cache is not None
    L = x.shape[1]
    sgu_states, pool_states = [], []
    for i in range(DEPTH):
        mod = jax.nn.silu(c) @ w_ada[i] + b_ada[i]
        sh1, sc1, gt1, sh2, sc2, gt2 = jnp.split(mod[:, None, :], N_MOD, axis=-1)
        h = rmsnorm(x, g_norm[i, 0]) * (1 + sc1) + sh1
        j = i // N_MIXERS
        if i % N_MIXERS == 0:
            T = L if sample else SGU_LEN
            mix, v = sgu_mixer(h, sgu_w_in[j], sgu_g_v[j], sgu_w_s[j], sgu_b_s[j], sgu_w_out[j], T)
            if sample:
                sgu_states.append(v)
        else:
            h_ext = jnp.concatenate([pool_cache[j].astype(h.dtype), h], axis=1) if sample else h
            mix = pool_mixer(h_ext, L, pool_w_grp[j], pool_scale[j])
            pool_states.append(h_ext[:, h_ext.shape[1] - POOL_STATE:])
        x = x + gt1 * rmsnorm(mix, g_norm[i, 1])
        h = rmsnorm(x, g_norm[i, 2]) * (1 + sc2) + sh2
        f = jnp.square(jax.nn.relu(h @ ffn_w_up[i])) @ ffn_w_down[i]
        x = x + gt2 * rmsnorm(f, g_norm[i, 3])
    return x, sgu_states, pool_states


def setup_inputs(seed: int = 0) -> dict:
    key = jax.random.key(seed)
    ks = jax.random.split(key, 20)
    f32 = jnp.float32
    nrm = lambda k, shape, s=1.0: (jax.random.normal(k, shape, f32) * s)
    return {
        "x_prompt": nrm(ks[0], (BATCH, SEQ, D_MODEL)),
        "x_sample": nrm(ks[1], (DEC_BATCH, DEC_SEQ, D_MODEL)),
        "cache_pool": nrm(ks[2], (N_POOL_LAYERS, DEC_BATCH, POOL_STATE, D_MODEL)),
        "c_prompt": nrm(ks[3], (BATCH, D_MODEL)),
        "c_sample": nrm(ks[4], (DEC_BATCH, D_MODEL)),
        "g_norm": 1.0 + nrm(ks[5], (DEPTH, 4, D_MODEL), 0.05),
        "w_ada": nrm(ks[6], (DEPTH, D_MODEL, N_MOD * D_MODEL), 0.5 * D_MODEL ** -0.5),
        "b_ada": nrm(ks[7], (DEPTH, N_MOD * D_MODEL), 0.02),
        "sgu_w_in": nrm(ks[8], (N_SGU_LAYERS, D_MODEL, 2 * D_SGU), D_MODEL ** -0.5),
        "sgu_g_v": 1.0 + nrm(ks[9], (N_SGU_LAYERS, D_SGU), 0.05),
        "sgu_w_s": nrm(ks[10], (N_SGU_LAYERS, SGU_GROUPS, SGU_LEN, SGU_LEN), SGU_LEN ** -0.5),
        "sgu_b_s": 1.0 + nrm(ks[11], (N_SGU_LAYERS, SGU_GROUPS, SGU_LEN), 0.1),
        "sgu_w_out": nrm(ks[12], (N_SGU_LAYERS, D_SGU, D_MODEL), D_SGU ** -0.5),
        "pool_w_grp": nrm(ks[13], (N_POOL_LAYERS, POOL_GROUPS, POOL_GROUP_DIM, POOL_GROUP_DIM), POOL_GROUP_DIM ** -0.5),
        "pool_scale": 1.0 + nrm(ks[14], (N_POOL_LAYERS, D_MODEL), 0.1),
        "ffn_w_up": nrm(ks[15], (DEPTH, D_MODEL, D_FF), D_MODEL ** -0.5),
        "ffn_w_down": nrm(ks[16], (DEPTH, D_FF, D_MODEL), D_FF ** -0.5),
    }


def reference(x_prompt, x_sample, cache_pool, c_prompt, c_sample, g_norm, w_ada, b_ada,
              sgu_w_in, sgu_g_v, sgu_w_s, sgu_b_s, sgu_w_out, pool_w_grp, pool_scale,
              ffn_w_up, ffn_w_down):
    y_prompt, _, pool_p = trunk(x_prompt, c_prompt, None, g_norm, w_ada, b_ada, sgu_w_in,
                                sgu_g_v, sgu_w_s, sgu_b_s, sgu_w_out, pool_w_grp, pool_scale,
                                ffn_w_up, ffn_w_down)
    y_sample, sgu_s, pool_s = trunk(x_sample, c_sample, cache_pool, g_norm, w_ada, b_ada, sgu_w_in,
                                    sgu_g_v, sgu_w_s, sgu_b_s, sgu_w_out, pool_w_grp, pool_scale,
                                    ffn_w_up, ffn_w_down)
    state_pool_prompt = jnp.stack(pool_p, axis=0)
    state_sgu_sample = jnp.stack(sgu_s, axis=0)
    state_pool_sample = jnp.stack(pool_s, axis=0)
    return (y_prompt, y_sample, state_pool_prompt, state_sgu_sample, state_pool_sample)
```

```python
import numpy as np
import concourse.bass as bass
import concourse.mybir as mybir
from concourse.bass_utils import run_bass_kernel_spmd

F32 = mybir.dt.float32
BF16 = mybir.dt.bfloat16
AF = mybir.ActivationFunctionType
ALU = mybir.AluOpType

ENGS = ("pe", "act", "dve", "pool", "sp")


class Prog:
    def __init__(self, nc):
        self.nc = nc
        self.q = {e: [] for e in ENGS}
        self.esem = {e: nc.alloc_semaphore(name="sem_" + e) for e in ENGS}
        self.cnt = {e: 0 for e in ENGS}
        self.known = {e: {} for e in ENGS}
        self.keys = {}
        self.dcnt = {}
        self.semobj = {}
        for e in ENGS:
            self.semobj[self.esem[e].num] = self.esem[e]
        self.nwaits = 0

    def new_sem(self, name):
        s = self.nc.alloc_semaphore(name=name)
        self.semobj[s.num] = s
        self.dcnt[s.num] = 0
        return s

    def _deps(self, eng, reads, writes):
        need = {}
        for k in reads:
            st = self.keys.get(k)
            if st:
                for s, v in st["w"].items():
                    if need.get(s, 0) < v:
                        need[s] = v
        for k in writes:
            st = self.keys.get(k)
            if st:
                for d in (st["w"], st["r"]):
                    for s, v in d.items():
                        if need.get(s, 0) < v:
                            need[s] = v
        kn = self.known[eng]
        out = []
        for s, v in need.items():
            if eng == "pe" and s == self.esem["pe"].num:
                continue
            if kn.get(s, 0) >= v:
                continue
            kn[s] = v
            out.append((self.semobj[s], v))
        self.nwaits += len(out)
        return out

    def _commit(self, tok, reads, writes):
        s, v = tok
        for k in reads:
            st = self.keys.setdefault(k, {"w": {}, "r": {}})
            if st["r"].get(s, 0) < v:
                st["r"][s] = v
        for k in writes:
            self.keys[k] = {"w": {s: v}, "r": {}}

    def op(self, eng, fn, reads=(), writes=()):
        waits = self._deps(eng, reads, writes)
        self.cnt[eng] += 1
        tok = (self.esem[eng].num, self.cnt[eng])
        self.q[eng].append((waits, fn, (self.esem[eng], 1)))
        self._commit(tok, reads, writes)
        return tok

    def dma(self, eng, fn, sem, reads=(), writes=(), final=None):
        waits = self._deps(eng, reads, writes)
        self.dcnt[sem.num] += 16
        v = self.dcnt[sem.num] if final is None else final
        tok = (sem.num, v)
        self.q[eng].append((waits, fn, (sem, 16)))
        self._commit(tok, reads, writes)
        return tok

    def wait_keys(self, eng, keys):
        waits = self._deps(eng, (), keys)
        if waits:
            self.q[eng].append((waits, None, None))

    def emit(self):
        nc = self.nc
        with nc.Block() as block:
            def mk(name):
                def body(e):
                    for waits, fn, inc in self.q[name]:
                        for s, v in waits:
                            e.wait_ge(s, v)
                        if fn is None:
                            continue
                        ins = fn(e)
                        ins.then_inc(inc[0], inc[1])
                return body
            block.tensor(mk("pe"))
            block.scalar(mk("act"))
            block.vector(mk("dve"))
            block.gpsimd(mk("pool"))
            block.sync(mk("sp"))


D = 1024
NT = 512
NSLOT = 3
NSTREAMS = 2
SKEW = 3
LEAD0, LEADMIN, LEADMAX, FILL = 24, 2, 64, 3
DBG = {}
EPS = 1e-6
SH1, GM1, GG1, SH2, GM2, GG2 = range(6)


def build_program(SEQ, n_layers=4, do_sample=True):
    nc = bass.Bass("TRN2", target_bir_lowering=False)
    P = Prog(nc)

    def din(name, shape, dt=F32):
        return nc.dram_tensor(name, shape, dt, kind="ExternalInput").ap()

    def dout(name, shape, dt=F32):
        return nc.dram_tensor(name, shape, dt, kind="ExternalOutput").ap()

    def dscr(name, shape, dt=BF16):
        return nc.dram_tensor(name, shape, dt, kind="Internal").ap()

    xp = din("xp", [SEQ, D]); xs = din("xs", [64, D]); cache = din("cache", [2, 30, D])
    rows0 = din("rows0", [21, D])
    w_ada = din("w_ada", [4, D, 6 * D]); b_ada = din("b_ada", [4, 6 * D])
    w_in = din("w_in", [2, D, 4096]); g_v = din("g_v", [2, 2048]); w_s = din("w_s", [2, 8, 128, 128])
    b_s = din("b_s", [1, 2048]); w_out = din("w_out", [2, 2048, D]); pw = din("pw", [2, 4, 256, 256])
    w_up = din("w_up", [4, D, 4096]); w_dn = din("w_dn", [4, 4096, D])
    ident_d = din("ident", [128, 128]); rcnt_d = din("rcnt", [128, 64])
    yp = dout("yp", [SEQ, D]); ys = dout("ys", [64, D]); spp = dout("spp", [2, 15, D])
    ssg = dout("ssg", [2, 64, 2048]); sps = dout("sps", [2, 30, D])
    wb_in = dscr("wb_in", [2, 8, 128, 4096]); wb_out = dscr("wb_out", [2, 4, 128, 4096])
    wb_up = dscr("wb_up", [4, 8, 128, 4096]); wb_dn = dscr("wb_dn", [4, 8, 128, 4096])
    wb_pw = dscr("wb_pw", [2, 128, 2048])

    A = nc.alloc_sbuf_tensor
    ident = A("ident_sb", [128, 128], F32)
    onesd = A("onesd", [128, 128], BF16)
    ones2 = A("ones2", [2, 128], BF16)
    rcnt = A("rcnt_sb", [128, 4, 16], F32)
    epsb = A("epsb", [128, 1], F32)
    nhalf = A("nhalf", [128, 1], F32)
    FM0 = A("FM0", [128, 8, 21], F32)
    SC = A("SC", [128, 8, 3], F32)
    CST = A("CST", [128, 4, 6, 8, 3], F32)
    wsT = A("wsT", [128, 2, 8, 128], BF16)
    wsTs = A("wsTs", [64, 2, 8, 64], BF16)
    BI = A("BI", [2, 2, 8, 128], BF16)
    BIs = A("BIs", [2, 2, 8, 64], BF16)
    halo = A("halo", [128, 4, 8, 15], F32)
    rstd = A("rstd", [128, 1, NT], F32)
    vst = A("vst", [128, 8], F32)
    xt = A("xt", [128, 8, NT], F32)
    T1 = A("T1", [128, 8 * NT], F32)
    hbf = A("hbf", [128, 8, NT], BF16)
    BIGA = A("BIGA", [128, 8192], F32)
    BIGB = A("BIGB", [128, 12288], F32)
    WR = [A("wr%d" % i, [128, 2048], F32) for i in range(NSLOT * NSTREAMS)]
    stage = A("stage", [128, 2, D], F32)
    sq2 = A("sq2", [128, 8, NT], BF16)
    vpart = A("vpart", [128, 16], F32)
    rtmp = A("rtmp", [128, NT], F32)
    fix = A("fix", [128, 2, 16], F32)
    PS = [nc.alloc_psum_tensor("ps%d" % i, [128, 512], F32) for i in range(8)]

    def Akeys(lo, hi):
        return ["A%d" % i for i in range(lo, hi)]
    T1K = ["T1pre"]

    psi = [0]

    def next_ps():
        b = psi[0]
        psi[0] = (b + 1) % 8
        return b

    semc = [0]

    def fresh_sem():
        semc[0] += 1
        return P.new_sem("d%d" % semc[0])

    def dma1(eng, fn, reads=(), writes=()):
        return P.dma(eng, fn, fresh_sem(), reads=reads, writes=writes)

    final_keys = []

    class Ring:
        def __init__(self, rid, slot_ids):
            self.rid = rid
            self.slot_ids = slot_ids
            self.ns = len(slot_ids)
            self.blocks = []
            self.issued = 0
            self.consumed = 0
            self.sems = [P.new_sem("ring%s_%d" % (rid, i)) for i in range(self.ns)]

        def add(self, src, dt, rkeys):
            self.blocks.append((src, dt, rkeys))

        def view(self, slot, dt):
            t = WR[self.slot_ids[slot]]
            return t[:] if dt == F32 else t[:].bitcast(BF16)

        def key(self, slot):
            return "WR%d" % self.slot_ids[slot]

        def try_issue(self):
            while self.issued < len(self.blocks) and self.issued < self.consumed + self.ns:
                b = self.issued
                src, dt, rkeys = self.blocks[b]
                slot = b % self.ns
                dst = self.view(slot, dt)
                if len(src.shape) == 3:
                    dst = dst.rearrange("p (k c) -> p k c", k=src.shape[1])
                elif src.shape[-1] != dst.shape[-1]:
                    dst = dst[:, 0:src.shape[-1]]
                P.dma("sp", lambda e, dst=dst, src=src: e.dma_start(out=dst, in_=src), self.sems[slot],
                      reads=rkeys, writes=[self.key(slot)])
                self.issued += 1

    nwr = NSLOT * NSTREAMS
    rings = [Ring(str(i), list(range(i * NSLOT, (i + 1) * NSLOT))) for i in range(NSTREAMS)]
    ring_pre = Ring("pre", list(range(nwr)))
    ring_smp = Ring("smp", list(range(nwr)))

    def ring_get(st, dt):
        r = st["ring"]
        k = st["blk"]
        assert k == r.consumed, ("ring order", k, r.consumed)
        r.try_issue()
        assert k < r.issued
        assert r.blocks[k][1] == dt
        slot = k % r.ns
        return r.view(slot, dt), r.key(slot)

    def ring_done(st):
        r = st["ring"]
        r.consumed += 1
        st["blk"] += 1
        r.try_issue()

    for l in range(n_layers):
        for cb in range(24):
            ring_pre.add(w_ada[l][:, cb * 256:(cb + 1) * 256].rearrange("(k p) c -> p k c", p=128), F32, [])

    def add_tile_blocks(r):
        n0 = len(r.blocks)
        for l in range(n_layers):
            j = l // 2
            ck = ["cv%d" % l]
            if l % 2 == 0:
                for b in range(8):
                    r.add(wb_in[j, b], BF16, ck)
                r.add(g_v[j:j + 1, :].partition_broadcast(128), F32, [])
                for b in range(4):
                    r.add(wb_out[j, b], BF16, ck)
            else:
                r.add(wb_pw[j], BF16, ck)
            for b in range(8):
                r.add(wb_up[l, b], BF16, ck)
            for b in range(8):
                r.add(wb_dn[l, b], BF16, ck)
        return len(r.blocks) - n0
    BPT = 0
    for r in rings:
        for ti in range(SEQ // NT):
            BPT = add_tile_blocks(r)
    if do_sample:
        BPT = add_tile_blocks(ring_smp)

    conv = []
    for l in range(n_layers):
        j = l // 2
        lst = []
        if l % 2 == 0:
            for b in range(8):
                lst.append((wb_in[j, b].rearrange("p (k c) -> p k c", k=8),
                            w_in[j][:, b * 512:(b + 1) * 512].rearrange("(k p) c -> p k c", p=128)))
            for b in range(4):
                lst.append((wb_out[j, b].rearrange("p (k c) -> p k c", k=16),
                            w_out[j][:, b * 256:(b + 1) * 256].rearrange("(k p) c -> p k c", p=128)))
        else:
            lst.append((wb_pw[j].rearrange("p (g k d) -> p g k d", g=4, k=2),
                        pw[j].rearrange("g (k p) d -> p g k d", p=128)))
        for b in range(8):
            lst.append((wb_up[l, b].rearrange("p (k c) -> p k c", k=8),
                        w_up[l][:, b * 512:(b + 1) * 512].rearrange("(k p) c -> p k c", p=128)))
        for b in range(8):
            lst.append((wb_dn[l, b].rearrange("p (k c) -> p k c", k=32),
                        w_dn[l][:, b * 128:(b + 1) * 128].rearrange("(k p) c -> p k c", p=128)))
        conv.append(lst)

    dma1("sp", lambda e: e.dma_start(out=ident[:], in_=ident_d[:, :]), writes=["ident"])
    dma1("sp", lambda e: e.dma_start(out=rcnt[:].rearrange("p g t -> p (g t)"), in_=rcnt_d[:, :]), writes=["rcnt"])
    R0 = T1[0:21, 0:D]
    dma1("sp", lambda e: e.dma_start(out=R0, in_=rows0[:, :]), writes=T1K)
    R1 = BIGB[0:4, 0:6144]
    dma1("sp", lambda e: e.dma_start(out=R1, in_=b_ada[:, :]), writes=["BIGB"])
    WS32 = BIGA[:, 0:2048].rearrange("p (l g j) -> p l g j", l=2, g=8)
    for l in range(2):
        dma1("sp", lambda e, l=l: e.dma_start(out=WS32[:, l], in_=w_s[l].rearrange("g i j -> i g j")), writes=["WS32_%d" % l])
    WSS = BIGA[0:64, 2048:3072].rearrange("p (l g j) -> p l g j", l=2, g=8)
    P.op("pool", lambda e: e.memset(BIGA[0:64, 2048:3072], 0.0), writes=["WSS"])
    for l in range(2):
        for b in range(2):
            dma1("sp", lambda e, l=l, b=b: e.dma_start(
                out=WSS[32 * b:32 * b + 32, l, :, 32 * b:32 * b + 32],
                in_=w_s[l][:, 0:32, 0:32].rearrange("g i j -> i g j")), reads=["WSS"], writes=["WSSd%d%d" % (l, b)])
    BS32 = BIGA[0:1, 3072:5120]
    dma1("sp", lambda e: e.dma_start(out=BS32, in_=b_s[:, :]), writes=["BS32"])

    P.op("pool", lambda e: e.memset(onesd[:], 1.0 / 1024.0), writes=["onesd"])
    P.op("pool", lambda e: e.memset(ones2[:], 1.0), writes=["ones2"])
    P.op("pool", lambda e: e.memset(epsb[:], EPS), writes=["epsb"])
    P.op("pool", lambda e: e.memset(nhalf[:], -0.5), writes=["nhalf"])

    conv_sems = [P.new_sem("cv%d" % l) for l in range(n_layers)]

    def issue_conv(l):
        tot = 16 * len(conv[l])
        for (o, i) in conv[l]:
            P.q["pool"].append(([], (lambda e, o=o, i=i: e.dma_start(out=o, in_=i)), (conv_sems[l], 16)))
        P._commit((conv_sems[l].num, tot), [], ["cv%d" % l])
    issue_conv(0)

    b0 = next_ps()

    def f(e):
        for c in range(8):
            i = e.transpose(PS[b0][:, c * 21:(c + 1) * 21], R0[:, c * 128:(c + 1) * 128], ident[0:21, 0:21])
        return i
    P.op("pe", f, reads=T1K + ["ident"], writes=["ps%d" % b0])
    P.op("dve", lambda e: e.tensor_copy(out=FM0[:], in_=PS[b0][:, 0:168].rearrange("p (c r) -> p c r", c=8)),
         reads=["ps%d" % b0], writes=["FM0"])
    P.op("act", lambda e: e.activation(out=SC[:], in_=FM0[:, :, 16:19], func=AF.Silu), reads=["FM0"], writes=["SC"])
    BADA = BIGA[:, 5120:5312].rearrange("p (j l) -> p j l", j=48)
    b1 = next_ps()

    def f(e):
        for jj in range(48):
            i = e.transpose(PS[b1][:, jj * 4:(jj + 1) * 4], R1[:, jj * 128:(jj + 1) * 128], ident[0:4, 0:4])
        return i
    P.op("pe", f, reads=["BIGB", "ident"], writes=["ps%d" % b1])
    P.op("dve", lambda e: e.tensor_copy(out=BADA, in_=PS[b1][:, 0:192].rearrange("p (j l) -> p j l", j=48)),
         reads=["ps%d" % b1], writes=["BADA"])
    for l in range(2):
        for hh in range(2):
            bk = next_ps()

            def f(e, l=l, hh=hh, bk=bk):
                for g in range(4):
                    i = e.transpose(PS[bk][:, g * 128:(g + 1) * 128], WS32[:, l, hh * 4 + g, :], ident[:])
                return i
            P.op("pe", f, reads=["WS32_%d" % l, "ident"], writes=["ps%d" % bk])
            P.op("dve", lambda e, l=l, hh=hh, bk=bk: e.tensor_copy(
                out=wsT[:, l, hh * 4:(hh + 1) * 4, :], in_=PS[bk][:].rearrange("p (g i) -> p g i", g=4)),
                reads=["ps%d" % bk], writes=["wsT"])
        bk = next_ps()

        def f(e, l=l, bk=bk):
            for g in range(8):
                i = e.transpose(PS[bk][0:64, g * 64:(g + 1) * 64], WSS[:, l, g, :], ident[0:64, 0:64])
            return i
        P.op("pe", f, reads=["WSS", "WSSd%d0" % l, "WSSd%d1" % l, "ident"], writes=["ps%d" % bk])
        P.op("dve", lambda e, l=l, bk=bk: e.tensor_copy(
            out=wsTs[:, l, :, :], in_=PS[bk][0:64, :].rearrange("p (g i) -> p g i", g=8)),
            reads=["ps%d" % bk], writes=["wsTs"])
    P.op("dve", lambda e: e.memset(wsT[64:128, :, :, 0:64], 0.0), reads=["wsT"], writes=["wsT"])
    BHI = BIGA[0:1, 5312:6336].bitcast(BF16)
    BLO = BIGA[0:1, 6336:7360].bitcast(BF16)
    BH32 = BIGA[0:1, 3072 + 2048 + 2240:3072 + 2048 + 2240 + 1]
    BTMP = BIGB[0:1, 6144:8192]
    P.op("dve", lambda e: e.tensor_copy(out=BHI, in_=BS32), reads=["BS32"], writes=["BHI"])
    P.op("dve", lambda e: e.tensor_copy(out=BTMP, in_=BHI), reads=["BHI"], writes=["BTMP"])
    P.op("dve", lambda e: e.tensor_tensor(out=BLO, in0=BS32, in1=BTMP, op=ALU.subtract), reads=["BS32", "BTMP"], writes=["BLO"])
    for r, (src, key) in enumerate(((BHI, "BHI"), (BLO, "BLO"))):
        dma1("sp", lambda e, r=r, src=src: e.dma_start(out=BI[r:r + 1].rearrange("p l g i -> p (l g i)"), in_=src),
             reads=[key], writes=["BI%d" % r])
        for hh in range(2):
            dma1("sp", lambda e, r=r, src=src, hh=hh: e.dma_start(
                out=BIs[r:r + 1, :, :, hh * 32:(hh + 1) * 32],
                in_=src.rearrange("p (l g i) -> p l g i", l=2, g=8)[:, :, :, 0:32]),
                reads=[key], writes=["BIs%d%d" % (r, hh)])
    BIK = ["BI0", "BI1"]
    BISK = ["BIs00", "BIs01", "BIs10", "BIs11"]

    evi = [0]

    def evac_copy(dst, src, reads, writes):
        evi[0] ^= 1
        if evi[0]:
            P.op("act", lambda e: e.activation(out=dst, in_=src, func=AF.Copy), reads=reads, writes=writes)
        else:
            P.op("dve", lambda e: e.tensor_copy(out=dst, in_=src), reads=reads, writes=writes)

    MOD = BIGA[:, 7360:7936].rearrange("p (l j t) -> p l j t", l=4, j=48)
    pre_st = {"blk": 0, "ring": ring_pre}
    MODT = BIGB[0:3, 6144:12288]
    MODTK = ["MODT%d" % q for q in range(12)]
    for l in range(n_layers):
        for cb2 in range(12):
            bkt = next_ps()
            for half in range(2):
                wv, wk = ring_get(pre_st, F32)
                wv3 = wv.rearrange("p (k c) -> p k c", k=8)

                def f(e, wv3=wv3, half=half, bkt=bkt):
                    for k in range(8):
                        i = e.matmul(PS[bkt][0:3, half * 256:(half + 1) * 256], lhsT=SC[:, k, :], rhs=wv3[:, k, :],
                                     start=(k == 0), stop=(k == 7))
                    return i
                P.op("pe", f, reads=[wk, "SC"], writes=["ps%d" % bkt] if half == 0 else [])
                if half:
                    P._commit((P.esem["pe"].num, P.cnt["pe"]), [], ["ps%d" % bkt])
                ring_done(pre_st)
            evac_copy(MODT[:, cb2 * 512:(cb2 + 1) * 512], PS[bkt][0:3, :], ["ps%d" % bkt], [MODTK[cb2], "BTMP"])
        bk = next_ps()

        def f(e, bk=bk):
            for jx in range(48):
                i = e.transpose(PS[bk][:, jx * 3:(jx + 1) * 3], MODT[:, jx * 128:(jx + 1) * 128], ident[0:3, 0:3])
            return i
        P.op("pe", f, reads=MODTK + ["ident"], writes=["ps%d" % bk])
        P.op("dve", lambda e, l=l, bk=bk: e.tensor_tensor(
            out=MOD[:, l], in0=PS[bk][:, 0:144].rearrange("p (j t) -> p j t", j=48),
            in1=BADA[:, :, l:l + 1].to_broadcast([128, 48, 3]), op=ALU.add),
            reads=["ps%d" % bk, "BADA"], writes=["MOD%d" % l])

        def gb(k, l=l):
            return FM0[:, :, 4 * l + k:4 * l + k + 1].to_broadcast([128, 8, 3])
        P.op("dve", lambda e, l=l: e.tensor_copy(out=CST[:, l, SH1], in_=MOD[:, l, 0:8, :]), reads=["MOD%d" % l], writes=["CST"])
        P.op("dve", lambda e, l=l, gb=gb: e.scalar_tensor_tensor(out=CST[:, l, GM1], in0=MOD[:, l, 8:16, :], scalar=1.0, in1=gb(0),
                                                                   op0=ALU.add, op1=ALU.mult), reads=["MOD%d" % l, "FM0"], writes=["CST"])
        P.op("dve", lambda e, l=l, gb=gb: e.tensor_tensor(out=CST[:, l, GG1], in0=MOD[:, l, 16:24, :], in1=gb(1), op=ALU.mult),
             reads=["MOD%d" % l, "FM0"], writes=["CST"])
        P.op("dve", lambda e, l=l: e.tensor_copy(out=CST[:, l, SH2], in_=MOD[:, l, 24:32, :]), reads=["MOD%d" % l], writes=["CST"])
        P.op("dve", lambda e, l=l, gb=gb: e.scalar_tensor_tensor(out=CST[:, l, GM2], in0=MOD[:, l, 32:40, :], scalar=1.0, in1=gb(2),
                                                                   op0=ALU.add, op1=ALU.mult), reads=["MOD%d" % l, "FM0"], writes=["CST"])
        P.op("dve", lambda e, l=l, gb=gb: e.tensor_tensor(out=CST[:, l, GG2], in0=MOD[:, l, 40:48, :], in1=gb(3), op=ALU.mult),
             reads=["MOD%d" % l, "FM0"], writes=["CST"])

    P.wait_keys("pool", ["MOD%d" % (n_layers - 1)])
    for l in range(1, n_layers):
        issue_conv(l)
    PRE_KEYS = ["BIGB", "WS32_0", "WS32_1", "WSS", "WSSd00", "WSSd01", "WSSd10", "WSSd11", "BS32", "BHI", "BLO", "BTMP",
                "BADA"] + ["MOD%d" % l for l in range(n_layers)] + MODTK
    for eng in ("act", "dve", "pool", "pe"):
        P.wait_keys(eng, PRE_KEYS + T1K)

    CH_ENG = ["dve", "pool", "dve", "dve", "pool", "dve", "dve", "pool"]
    gpi = [0]

    def next_ps():
        b = gpi[0]
        gpi[0] = (b + 1) % 6
        return b

    def mk_stream(name, kind, c0, n, s0, NS, TS, sidx, segs, t1off, vbfoff, a0, bb0, slots):
        st = dict(name=name, kind=kind, c0=c0, n=n, s0=s0, NS=NS, TS=TS, sidx=sidx, segs=segs, slots=slots, rb=0, sb=6 + sidx,
                  ring=(ring_smp if kind == "s" else rings[sidx if NSTREAMS > 1 else 0]))
        st["junk"] = rtmp[:, c0:c0 + n].bitcast(BF16) if n == 256 else rtmp[:].bitcast(BF16)[:, 0:512]
        st["T1v"] = T1[:, t1off:t1off + 8 * n].rearrange("p (c n) -> p c n", c=8)
        st["vbf"] = T1[:, vbfoff:vbfoff + NS * 1024].bitcast(BF16).rearrange("p (s n) -> p s n", s=NS)
        st["u32"] = BIGA[:, a0:a0 + 16 * n].rearrange("p (c n) -> p c n", c=16)
        st["abf"] = BIGA[:, a0:a0 + 16 * n].bitcast(BF16).rearrange("p (c n) -> p c n", c=32)
        st["vbig"] = BIGB[:, bb0:bb0 + NS * 2048].rearrange("p (s n) -> p s n", s=NS)
        st["usb"] = BIGB[:, bb0 + NS * 2048:bb0 + NS * 2048 + 8 * n].bitcast(BF16).rearrange("p (c n) -> p c n", c=16)
        E = 94 if kind == "s" else n + 15
        st["E"] = E
        st["hp"] = BIGB[:, bb0:bb0 + 8 * E].rearrange("p (c e) -> p c e", c=8)
        st["W1"] = BIGB[:, bb0 + 8 * E:bb0 + 16 * E].rearrange("p (c e) -> p c e", c=8)
        st["W2"] = BIGB[:, bb0 + 16 * E:bb0 + 22 * E].rearrange("p (c e) -> p c e", c=6)
        for nm in ("T1", "x", "h", "q", "hp"):
            st[nm + "K"] = ["%s%s_%d" % (nm, name, c) for c in range(8)]
        st["AK"] = ["A%s_%d" % (name, i) for i in range(32)]
        st["USK"] = ["us%s_%d" % (name, i) for i in range(16)]
        st["bbkeys"] = set()
        return st

    def X(st, c):
        return xt[:, c, st["c0"]:st["c0"] + st["n"]]

    def SQ(st, c):
        return sq2[:, c, st["c0"]:st["c0"] + st["n"]]

    def H(st, c, a=0, w=None):
        w = st["n"] - a if w is None else w
        return hbf[:, c, st["c0"] + a:st["c0"] + a + w]

    def stats_mm(st, c, first, last):
        if not last:
            return
        n, sb = st["n"], st["sb"]

        def f(e):
            for cc in range(8):
                i = e.matmul(PS[sb][:, 0:n], lhsT=onesd[:], rhs=SQ(st, cc), start=(cc == 0), stop=(cc == 7))
            return i
        P.op("pe", f, reads=st["qK"] + ["onesd"], writes=["ps%d" % sb])

    def finish_rstd(st):
        n, sb, rb = st["n"], st["sb"], st["rb"]
        rk = "rstd%s%d" % (st["name"], rb)
        ap = rstd[:, rb, st["c0"]:st["c0"] + n]
        if DBG.get("lnexp"):
            P.op("act", lambda e: e.activation(out=ap, in_=PS[sb][:, 0:n], func=AF.Ln, bias=epsb[:, 0:1], scale=1.0),
                 reads=["ps%d" % sb, "epsb"], writes=[rk])
            P.op("act", lambda e: e.activation(out=ap, in_=ap, func=AF.Exp, scale=-0.5), reads=[rk], writes=[rk])
            return ap, rk
        P.op("act", lambda e: e.activation(out=ap, in_=PS[sb][:, 0:n], func=AF.Sqrt, bias=epsb[:, 0:1], scale=1.0),
             reads=["ps%d" % sb, "epsb"], writes=[rk])
        P.op("dve", lambda e: e.reciprocal(out=ap, in_=ap), reads=[rk], writes=[rk])
        return ap, rk

    def x_squares(st):
        for c in range(8):
            P.op("act", lambda e, c=c: e.activation(out=SQ(st, c), in_=X(st, c), func=AF.Square), reads=[st["xK"][c]], writes=[st["qK"][c]])

    def pre_norm(st, l, kg, ks, dst_fn, dkeys):
        rs, rk = finish_rstd(st)
        T1v = st["T1v"]
        for c in range(8):
            P.op(CH_ENG[c], lambda e, c=c: e.tensor_tensor(out=T1v[:, c, :], in0=X(st, c), in1=rs, op=ALU.mult),
                 reads=[st["xK"][c], rk], writes=[st["T1K"][c]])
            for (a, w, t) in st["segs"]:
                P.op("act", lambda e, c=c, a=a, w=w, t=t: e.activation(
                    out=dst_fn(c, (a, w, t)), in_=T1v[:, c, a:a + w], func=AF.Identity,
                    scale=CST[:, l, kg, c, t:t + 1], bias=CST[:, l, ks, c, t:t + 1]),
                    reads=[st["T1K"][c], "CST"], writes=[dkeys[c]])

    def evac_mix(st, l, kgg, bk, m, sc=None):
        n = st["n"]
        T1v = st["T1v"]
        if sc is None:
            P.op("act", lambda e: e.activation(out=SQ(st, m), in_=PS[bk][:, 0:n], func=AF.Square),
                 reads=["ps%d" % bk], writes=[st["qK"][m]])
        else:
            P.op("act", lambda e: e.activation(out=SQ(st, m), in_=PS[bk][:, 0:n], func=AF.Square, scale=sc),
                 reads=["ps%d" % bk, "FM0"], writes=[st["qK"][m]])
        for (a, w, t) in st["segs"]:
            if sc is None:
                P.op("dve", lambda e, a=a, w=w, t=t: e.tensor_scalar(out=T1v[:, m, a:a + w], in0=PS[bk][:, a:a + w],
                                                                     scalar1=CST[:, l, kgg, m, t:t + 1], scalar2=None, op0=ALU.mult),
                     reads=["ps%d" % bk, "CST", st["qK"][m]], writes=[st["T1K"][m]])
            else:
                P.op("dve", lambda e, a=a, w=w, t=t: e.tensor_scalar(out=T1v[:, m, a:a + w], in0=PS[bk][:, a:a + w],
                                                                     scalar1=sc, scalar2=CST[:, l, kgg, m, t:t + 1],
                                                                     op0=ALU.mult, op1=ALU.mult),
                     reads=["ps%d" % bk, "CST", "FM0", st["qK"][m]], writes=[st["T1K"][m]])

    def post_norm(st, l, want_sq):
        rs, rk = finish_rstd(st)
        T1v = st["T1v"]
        for c in range(8):
            E = CH_ENG[c]
            P.op(E, lambda e, c=c: e.tensor_tensor(out=T1v[:, c, :], in0=T1v[:, c, :], in1=rs, op=ALU.mult),
                 reads=[st["T1K"][c], rk], writes=[st["T1K"][c]])
            P.op(E, lambda e, c=c: e.tensor_tensor(out=X(st, c), in0=X(st, c), in1=T1v[:, c, :], op=ALU.add),
                 reads=[st["T1K"][c], st["xK"][c]], writes=[st["xK"][c]])
            if want_sq:
                P.op("act", lambda e, c=c: e.activation(out=SQ(st, c), in_=X(st, c), func=AF.Square),
                     reads=[st["xK"][c]], writes=[st["qK"][c]])

    def ffn_gen(st, l, last_layer):
        n, c0 = st["n"], st["c0"]
        abf = st["abf"]
        pre_norm(st, l, GM2, SH2, lambda c, sg: H(st, c, sg[0], sg[1]), st["hK"])
        yield "chain"
        for b in range(8):
            wv, wk = ring_get(st, BF16)
            for jj in range(4):
                j = b * 4 + jj
                bk = next_ps()

                def f(e, wv=wv, jj=jj, bk=bk):
                    for k in range(8):
                        i = e.matmul(PS[bk][:, 0:n], lhsT=wv[:, k * 512 + jj * 128:k * 512 + (jj + 1) * 128],
                                     rhs=H(st, k), start=(k == 0), stop=(k == 7))
                    return i
                P.op("pe", f, reads=[wk] + st["hK"], writes=["ps%d" % bk])
                rk = "rtmp%s" % st["name"]
                rt = rtmp[:, c0:c0 + n]
                if False:
                    P.op("dve", lambda e, j=j, bk=bk: e.scalar_tensor_tensor(out=abf[:, j, :], in0=PS[bk][:, 0:n], scalar=0.0,
                                                                             in1=PS[bk][:, 0:n], op0=ALU.max, op1=ALU.mult),
                         reads=["ps%d" % bk], writes=[st["AK"][j]])
                    continue
                P.op("act", lambda e, bk=bk, rt=rt: e.activation(out=rt, in_=PS[bk][:, 0:n], func=AF.Relu),
                     reads=["ps%d" % bk], writes=[rk])
                if DBG.get("relupool"):
                    P.op("pool", lambda e, j=j, rt=rt: e.tensor_tensor(out=abf[:, j, :], in0=rt, in1=rt, op=ALU.mult),
                         reads=[rk], writes=[st["AK"][j]])
                else:
                    P.op("act", lambda e, j=j, rt=rt: e.activation(out=abf[:, j, :], in_=rt, func=AF.Square),
                         reads=[rk], writes=[st["AK"][j]])
            ring_done(st)
            yield "blk"
        pend = None
        for m in range(8):
            wv, wk = ring_get(st, BF16)
            bk = next_ps()

            def f(e, wv=wv, bk=bk):
                for k in range(32):
                    i = e.matmul(PS[bk][:, 0:n], lhsT=wv[:, k * 128:(k + 1) * 128], rhs=abf[:, k, :],
                                 start=(k == 0), stop=(k == 31))
                return i
            P.op("pe", f, reads=[wk] + st["AK"], writes=["ps%d" % bk])
            evac_mix(st, l, GG2, bk, m)
            if pend is not None:
                stats_mm(st, pend, pend == 0, False)
            pend = m
            ring_done(st)
            yield "blk"
        stats_mm(st, 7, False, True)
        post_norm(st, l, not last_layer)
        if not last_layer:
            yield "chain"
            stats_mm(st, 7, False, True)

    def sgu_gen(st, l):
        j = l // 2
        n, c0, TS, NS, s0 = st["n"], st["c0"], st["TS"], st["NS"], st["s0"]
        smp = st["kind"] == "s"
        u32, vbig, vbf, usb = st["u32"], st["vbig"], st["vbf"], st["usb"]
        nm = st["name"]
        pre_norm(st, l, GM1, SH1, lambda c, sg: H(st, c, sg[0], sg[1]), st["hK"])
        yield "chain"
        for b in range(4):
            wv, wk = ring_get(st, BF16)
            for jj in range(4):
                ju = b * 4 + jj
                bk = next_ps()

                def f(e, wv=wv, jj=jj, bk=bk):
                    for k in range(8):
                        i = e.matmul(PS[bk][:, 0:n], lhsT=wv[:, k * 512 + jj * 128:k * 512 + (jj + 1) * 128],
                                     rhs=H(st, k), start=(k == 0), stop=(k == 7))
                    return i
                P.op("pe", f, reads=[wk] + st["hK"], writes=["ps%d" % bk])
                P.op("act", lambda e, ju=ju, bk=bk: e.activation(out=u32[:, ju, :], in_=PS[bk][:, 0:n], func=AF.Gelu_apprx_tanh),
                     reads=["ps%d" % bk], writes=st["AK"][2 * ju:2 * ju + 2])
            ring_done(st)
            yield "blk"
        for nb in range(4):
            wv, wk = ring_get(st, BF16)
            for i in range(NS):
                s = s0 + i
                bk = next_ps()

                def f(e, wv=wv, i=i, bk=bk):
                    for k in range(8):
                        ins = e.matmul(PS[bk][0:TS, :], lhsT=H(st, k, i * TS, TS), rhs=wv[:, k * 512:(k + 1) * 512],
                                       start=(k == 0), stop=(k == 7))
                    return ins
                P.op("pe", f, reads=[wk] + st["hK"], writes=["ps%d" % bk])
                vk = "vb%d_%d" % (s, nb)
                st["bbkeys"].add(vk)
                P.op("act", lambda e, i=i, nb=nb, bk=bk: e.activation(out=vbig[0:TS, i, nb * 512:(nb + 1) * 512], in_=PS[bk][0:TS, :],
                                                                      func=AF.Gelu_apprx_tanh),
                     reads=["ps%d" % bk], writes=[vk])
                P.op("act", lambda e, i=i, nb=nb, s=s: e.activation(out=st['junk'][0:TS, :], in_=vbig[0:TS, i, nb * 512:(nb + 1) * 512],
                                                                    func=AF.Square, accum_out=vpart[0:TS, s * 4 + nb:s * 4 + nb + 1]),
                     reads=[vk], writes=["vp%d_%d" % (s, nb)])
            ring_done(st)
            yield "blk"
        VPK = ["vp%d_%d" % (s0 + i, q) for i in range(NS) for q in range(4)]
        vsk = "vst" + nm
        P.op("dve", lambda e: e.reduce_sum(out=vst[0:TS, s0:s0 + NS],
                                           in_=vpart[0:TS, s0 * 4:(s0 + NS) * 4].rearrange("p (s q) -> p s q", q=4),
                                           axis=mybir.AxisListType.X), reads=VPK, writes=[vsk])
        P.op("dve", lambda e: e.tensor_scalar(out=vst[0:TS, 4 + s0:4 + s0 + NS], in0=vst[0:TS, s0:s0 + NS], scalar1=1.0 / 2048.0,
                                              scalar2=EPS, op0=ALU.mult, op1=ALU.add), reads=[vsk], writes=["vrs" + nm])
        P.op("pool", lambda e: e.tensor_tensor(out=vst[0:TS, 4 + s0:4 + s0 + NS], in0=vst[0:TS, 4 + s0:4 + s0 + NS],
                                               in1=nhalf[0:TS, 0:1].to_broadcast([TS, NS]), op=ALU.pow),
             reads=["vrs" + nm, "nhalf"], writes=["vrs" + nm])
        gv, gk = ring_get(st, F32)
        for i in range(NS):
            s = s0 + i
            rd = ["vb%d_%d" % (s, q) for q in range(4)] + ["vrs" + nm, gk]
            if smp:
                vstate = stage[0:64].rearrange("p a d -> p (a d)")
                P.op("dve", lambda e: e.scalar_tensor_tensor(out=vstate, in0=vbig[0:64, 0, :], scalar=vst[0:64, 4:5], in1=gv[0:64, :],
                                                             op0=ALU.mult, op1=ALU.mult), reads=rd, writes=["stage0", "stage1"])
                dma1("sp", lambda e: e.dma_start(out=ssg[j], in_=vstate), reads=["stage0", "stage1"])
                final_keys.extend(["stage0", "stage1"])
                P.op("pool", lambda e: e.tensor_copy(out=vbf[0:64, 0, :], in_=vstate), reads=["stage0", "stage1"], writes=st["T1K"])
            else:
                P.op("dve", lambda e, i=i, s=s: e.scalar_tensor_tensor(out=vbf[0:TS, i, :], in0=vbig[0:TS, i, :],
                                                                       scalar=vst[0:TS, 4 + s:5 + s], in1=gv[0:TS, :],
                                                                       op0=ALU.mult, op1=ALU.mult), reads=rd, writes=st["T1K"])
        ring_done(st)
        yield "blk"
        yield "chain"
        for dj in range(16):
            g = dj // 2
            bk = next_ps()

            def f(e, dj=dj, g=g, bk=bk):
                for i in range(NS):
                    o = PS[bk][:, i * TS:(i + 1) * TS]
                    e.matmul(o, lhsT=vbf[0:TS, i, dj * 128:(dj + 1) * 128],
                             rhs=(wsTs[0:64, j, g, :] if smp else wsT[:, j, g, :]), start=True, stop=False)
                    ins = e.matmul(o, lhsT=ones2[:, :], rhs=(BIs[:, j, g, :] if smp else BI[:, j, g, :]), start=False, stop=True)
                return ins
            P.op("pe", f, reads=st["T1K"] + ["wsT", "wsTs", "ones2"] + BIK + BISK, writes=["ps%d" % bk])
            st["bbkeys"].add(st["USK"][dj])
            P.op("dve", lambda e, dj=dj, bk=bk: e.tensor_tensor(out=usb[:, dj, :], in0=PS[bk][:, 0:n], in1=u32[:, dj, :], op=ALU.mult),
                 reads=["ps%d" % bk] + st["AK"][2 * dj:2 * dj + 2], writes=[st["USK"][dj]])
        pend = None
        for b in range(4):
            wv, wk = ring_get(st, BF16)
            for mm in range(2):
                m = b * 2 + mm
                bk = next_ps()

                def f(e, wv=wv, mm=mm, bk=bk):
                    for k in range(16):
                        ins = e.matmul(PS[bk][:, 0:n], lhsT=wv[:, k * 256 + mm * 128:k * 256 + (mm + 1) * 128], rhs=usb[:, k, :],
                                       start=(k == 0), stop=(k == 15))
                    return ins
                P.op("pe", f, reads=[wk] + st["USK"], writes=["ps%d" % bk])
                evac_mix(st, l, GG1, bk, m)
                if pend is not None:
                    stats_mm(st, pend, pend == 0, False)
                pend = m
            ring_done(st)
            yield "blk"
        stats_mm(st, 7, False, True)
        post_norm(st, l, True)
        yield "chain"
        stats_mm(st, 7, False, True)

    def pool_gen(st, tl, l):
        j = l // 2
        n, c0, E = st["n"], st["c0"], st["E"]
        smp = st["kind"] == "s"
        nm = st["name"]
        nseg, L = (2, 32) if smp else (1, n)
        SL = L + 15
        hp, W1, W2 = st["hp"], st["W1"], st["W2"]
        hp4 = hp.rearrange("p c (b e) -> p c b e", b=nseg)
        hk = "hphalo" + nm
        for k_ in st["hpK"] + [hk] + ["W%d%sg%d" % (a, nm, g) for a in (1, 2) for g in range(4)]:
            st["bbkeys"].add(k_)

        def val(ap3, ca, cb):
            return ap3[:, ca:cb, :].rearrange("p c (b e) -> p c b e", b=nseg)[:, :, :, 15:SL]
        if smp:
            stg = stage[0:30, 1, :]
            dma1("sp", lambda e: e.dma_start(out=stg, in_=cache[j]), writes=["stage1"])
            bk = next_ps()

            def f(e):
                for c in range(8):
                    i = e.transpose(PS[bk][:, c * 30:(c + 1) * 30], stg[:, c * 128:(c + 1) * 128], ident[0:30, 0:30])
                return i
            P.op("pe", f, reads=["stage1", "ident"], writes=["ps%d" % bk])
            P.op("dve", lambda e: e.tensor_copy(out=hp4[:, :, :, 0:15],
                                                in_=PS[bk][:, 0:240].rearrange("p (c b r) -> p c b r", c=8, b=2)),
                 reads=["ps%d" % bk], writes=[hk])
        elif c0 == 0 and tl["first"]:
            P.op("pool", lambda e: e.memset(hp[:, :, 0:15], 0.0), writes=[hk])
        else:
            hsrc = (0 if c0 > 0 else 2) + j
            P.op("pool", lambda e: e.tensor_copy(out=hp[:, :, 0:15], in_=halo[:, hsrc]), reads=["halo%d" % hsrc], writes=[hk])

        def dst_fn(c, sg):
            a, w, t = sg
            if smp:
                return hp4[:, c, t - 1, 15:SL]
            return hp[:, c, 15 + a:15 + a + w]
        pre_norm(st, l, GM1, SH1, dst_fn, st["hpK"])
        if not smp:
            hdst = (2 if c0 + n == NT else 0) + j
            P.op("pool", lambda e: e.tensor_copy(out=halo[:, hdst], in_=hp[:, :, n:n + 15]), reads=st["hpK"], writes=["halo%d" % hdst])
        if smp or (tl["last"] and c0 + n == NT):
            for b in range(nseg):
                bk = next_ps()
                bk2 = next_ps()
                slot = b if smp else st["slots"][0]

                def f(e, b=b, bk=bk, bk2=bk2):
                    for c in range(8):
                        o = (PS[bk] if c < 4 else PS[bk2])[0:15, (c % 4) * 128:(c % 4 + 1) * 128]
                        i = e.transpose(o, hp4[:, c, b, L:SL], ident[:])
                    return i
                P.op("pe", f, reads=st["hpK"] + ["ident"], writes=["ps%d" % bk, "ps%d" % bk2])
                sk = "stage%d" % slot
                P.op("dve", lambda e, slot=slot, bk=bk: e.tensor_copy(out=stage[0:15, slot, 0:512], in_=PS[bk][0:15, :]),
                     reads=["ps%d" % bk], writes=[sk])
                P.op("dve", lambda e, slot=slot, bk2=bk2: e.tensor_copy(out=stage[0:15, slot, 512:1024], in_=PS[bk2][0:15, :]),
                     reads=["ps%d" % bk2, sk], writes=[sk])
                dsto = sps[j, b * 15:(b + 1) * 15, :] if smp else spp[j]
                dma1("sp", lambda e, slot=slot, dsto=dsto: e.dma_start(out=dsto, in_=stage[0:15, slot, :]), reads=[sk])
                final_keys.append(sk)
        RD = st["hpK"] + [hk]
        pv = hbf[:, :, c0:c0 + n].rearrange("p c (b e) -> p c b e", b=nseg)
        for g in (3, 2, 1, 0):
            E_ = "dve" if g in (3, 0) else "pool"
            ca, cb = 2 * g, 2 * g + 2
            k1, k2 = "W1%sg%d" % (nm, g), "W2%sg%d" % (nm, g)
            P.op(E_, lambda e, ca=ca, cb=cb: e.tensor_tensor(out=W1[:, ca:cb, 1:E], in0=hp[:, ca:cb, 1:E], in1=hp[:, ca:cb, 0:E - 1], op=ALU.add),
                 reads=RD, writes=[k1])
            src, srck, off = W1, k1, 0
            if g >= 1:
                P.op(E_, lambda e, ca=ca, cb=cb: e.tensor_tensor(out=W2[:, ca - 2:cb - 2, 3:E], in0=W1[:, ca:cb, 3:E], in1=W1[:, ca:cb, 1:E - 2],
                                                                 op=ALU.add), reads=[k1], writes=[k2])
                src, srck, off = W2, k2, 2
            if g >= 2:
                P.op(E_, lambda e, ca=ca, cb=cb: e.tensor_tensor(out=W1[:, ca:cb, 7:E], in0=W2[:, ca - 2:cb - 2, 7:E], in1=W2[:, ca - 2:cb - 2, 3:E - 4],
                                                                 op=ALU.add), reads=[k2, k1], writes=[k1])
                src, srck, off = W1, k1, 0
            if g >= 3:
                P.op(E_, lambda e, ca=ca, cb=cb: e.tensor_tensor(out=W2[:, ca - 2:cb - 2, 15:E], in0=W1[:, ca:cb, 15:E], in1=W1[:, ca:cb, 7:E - 8],
                                                                 op=ALU.add), reads=[k1, k2], writes=[k2])
                src, srck, off = W2, k2, 2
            P.op("dve", lambda e, g=g, src=src, off=off, ca=ca, cb=cb: e.scalar_tensor_tensor(
                out=pv[:, ca:cb], in0=val(src, ca - off, cb - off), scalar=1.0 / (2 << g), in1=val(hp, ca, cb),
                op0=ALU.mult, op1=ALU.subtract), reads=[srck] + RD, writes=st["hK"][ca:cb])
            if c0 == 0 and tl["first"] and not smp:
                P.op("dve", lambda e, g=g, src=src, off=off, ca=ca, cb=cb: e.tensor_tensor(
                    out=fix[:], in0=src[:, ca - off:cb - off, 15:31], in1=rcnt[:, g:g + 1, :].to_broadcast([128, 2, 16]), op=ALU.mult),
                    reads=[srck, "rcnt"], writes=["fix"])
                P.op("dve", lambda e, ca=ca, cb=cb: e.tensor_tensor(out=hbf[:, ca:cb, 0:16], in0=fix[:], in1=hp[:, ca:cb, 15:31],
                                                                    op=ALU.subtract),
                     reads=["fix"] + RD, writes=st["hK"][ca:cb])
        yield "chain"
        wv, wk = ring_get(st, BF16)
        pend = None
        for m in range(8):
            g, mm = m // 2, m % 2
            bk = next_ps()

            def f(e, g=g, mm=mm, bk=bk):
                for kc in range(2):
                    i = e.matmul(PS[bk][:, 0:n], lhsT=wv[:, (g * 2 + kc) * 256 + mm * 128:(g * 2 + kc) * 256 + (mm + 1) * 128],
                                 rhs=H(st, 2 * g + kc), start=(kc == 0), stop=(kc == 1))
                return i
            P.op("pe", f, reads=[wk, st["hK"][2 * g], st["hK"][2 * g + 1]], writes=["ps%d" % bk])
            evac_mix(st, l, GG1, bk, m, sc=FM0[:, m, 19 + j:20 + j])
            if pend is not None:
                stats_mm(st, pend, pend == 0, False)
            pend = m
        stats_mm(st, 7, False, True)
        ring_done(st)
        yield "blk"
        post_norm(st, l, True)
        yield "chain"
        stats_mm(st, 7, False, True)

    def load_stream(st, tl):
        n, c0, TS, NS, s0 = st["n"], st["c0"], st["TS"], st["NS"], st["s0"]
        for i in range(NS):
            slot = st["slots"][i % len(st["slots"])]
            sk = "stage%d" % slot
            r0 = tl["row0"] + (s0 + i) * 128
            src = xs[:, :] if st["kind"] == "s" else xp[r0:r0 + 128, :]
            P.dma("sp", lambda e, slot=slot, src=src: e.dma_start(out=stage[0:TS, slot, :], in_=src), stg_sems[slot], writes=[sk])
            yield "chain"
            for hh in range(2):
                bk = next_ps()

                def f(e, slot=slot, hh=hh, bk=bk):
                    for c in range(4):
                        ins = e.transpose(PS[bk][:, c * TS:(c + 1) * TS], stage[0:TS, slot, (hh * 4 + c) * 128:(hh * 4 + c + 1) * 128],
                                          ident[0:TS, 0:TS])
                    return ins
                P.op("pe", f, reads=[sk, "ident"], writes=["ps%d" % bk])
                evac_copy(xt[:, hh * 4:(hh + 1) * 4, c0 + i * TS:c0 + (i + 1) * TS], PS[bk][:, 0:4 * TS].rearrange("p (c t) -> p c t", c=4),
                          ["ps%d" % bk], st["xK"][hh * 4:(hh + 1) * 4])
        x_squares(st)

    def store_stream(st, tl):
        n, c0, TS, NS, s0 = st["n"], st["c0"], st["TS"], st["NS"], st["s0"]
        for i in range(NS):
            slot = st["slots"][i % len(st["slots"])]
            sk = "stage%d" % slot
            for hh in range(2):
                bk = next_ps()

                def f(e, i=i, hh=hh, bk=bk):
                    for c in range(4):
                        ins = e.transpose(PS[bk][0:TS, c * 128:(c + 1) * 128], xt[:, hh * 4 + c, c0 + i * TS:c0 + (i + 1) * TS], ident[:])
                    return ins
                P.op("pe", f, reads=st["xK"][hh * 4:(hh + 1) * 4] + ["ident"], writes=["ps%d" % bk])
                evac_copy(stage[0:TS, slot, hh * 512:(hh + 1) * 512], PS[bk][0:TS, :], ["ps%d" % bk] + ([sk] if hh else []), [sk])
            r0 = tl["row0"] + (s0 + i) * 128
            dst = ys[:, :] if st["kind"] == "s" else yp[r0:r0 + 128, :]
            P.dma("sp", lambda e, slot=slot, dst=dst: e.dma_start(out=dst, in_=stage[0:TS, slot, :]), stg_sems[slot], reads=[sk])
            final_keys.append(sk)

    evi = [0]

    def evac_copy(dst, src, reads, writes):
        evi[0] ^= 1
        if evi[0]:
            P.op("act", lambda e: e.activation(out=dst, in_=src, func=AF.Copy), reads=reads, writes=writes)
        else:
            P.op("dve", lambda e: e.tensor_copy(out=dst, in_=src), reads=reads, writes=writes)

    stg_sems = [P.new_sem("stg%d" % i) for i in range(2)]

    def stream_gen(st, tls, blk0_fn):
        for ti, tl in enumerate(tls):
            st["blk"] = blk0_fn(ti)
            yield from load_stream(st, tl)
            yield "chain"
            stats_mm(st, 7, False, True)
            for l in range(n_layers):
                for eng in ("act", "dve", "pool"):
                    P.wait_keys(eng, sorted(st["bbkeys"]))
                if l % 2 == 0:
                    yield from sgu_gen(st, l)
                else:
                    yield from pool_gen(st, tl, l)
                yield from ffn_gen(st, l, l == n_layers - 1)
            yield "chain"
            store_stream(st, tl)

    ptiles = [dict(row0=t * NT, first=(t == 0), last=(t == SEQ // NT - 1)) for t in range(SEQ // NT)]
    ADA = n_layers * 24
    if NSTREAMS == 1:
        sts = [mk_stream("P", "p", 0, NT, 0, 4, 128, 0, [(0, NT, 0)], 0, 0, 0, 0, [0, 1])]
    else:
        sts = [mk_stream("L", "p", 0, 256, 0, 2, 128, 0, [(0, 256, 0)], 0, 0, 0, 0, [0]),
               mk_stream("G", "p", 256, 256, 2, 2, 128, 1, [(0, 256, 0)], 2048, 2048, 4096, 6144, [1])]
    gens = [stream_gen(s_, ptiles, (lambda ti: ti * BPT)) for s_ in sts]

    def adv(g, k):
        for _ in range(k):
            try:
                next(g)
            except StopIteration:
                return True
        return False
    if len(gens) == 1:
        adv(gens[0], 1 << 30)
    else:
        cnt = [0, 0]
        done = [False, False]

        def allowed(i):
            if done[i]:
                return False
            if done[i ^ 1]:
                return True
            lead = cnt[0] - cnt[1]
            return lead < LEADMAX if i == 0 else lead > LEADMIN
        cur = 0
        started = False
        budget = None
        while not (done[0] and done[1]):
            if not started and (cnt[0] >= LEAD0 or done[0]):
                started = True
            if not allowed(cur):
                cur ^= 1
                budget = None
                if not allowed(cur):
                    raise RuntimeError("scheduler stuck")
            try:
                tag = next(gens[cur])
            except StopIteration:
                done[cur] = True
                cur ^= 1
                budget = None
                continue
            if tag == "blk":
                cnt[cur] += 1
                if budget is not None:
                    budget -= 1
            if not started:
                continue
            o = cur ^ 1
            if tag == "chain":
                if allowed(o):
                    cur = o
                    budget = FILL
            elif budget is not None and budget <= 0 and allowed(o):
                cur = o
                budget = None
    if do_sample:
        ss = mk_stream("S", "s", 0, 64, 0, 1, 64, 0, [(0, 32, 1), (32, 32, 2)], 2048, 0, 0, 0, [0])
        adv(stream_gen(ss, [dict(row0=0, first=False, last=False)], lambda ti: 0), 1 << 30)

    P.wait_keys("sp", list(dict.fromkeys(final_keys)))
    for eng in ("act", "dve", "pool", "pe"):
        P.wait_keys(eng, [k for s_ in sts for k in s_["xK"]])
    P.emit()
    return nc, P


def make_consts():
    ident = np.eye(128, dtype=np.float32)
    r = np.zeros((4, 16), np.float32)
    for g, w in enumerate((2, 4, 8, 16)):
        for t in range(16):
            r[g, t] = 1.0 / min(w, t + 1)
    rcnt = np.broadcast_to(r.reshape(1, 64), (128, 64)).copy()
    return ident, rcnt


def core_inputs(c, x_prompt, x_sample, cache_pool, c_prompt, c_sample, g_norm, w_ada, b_ada, sgu_w_in, sgu_g_v,
                sgu_w_s, sgu_b_s, sgu_w_out, pool_w_grp, pool_scale, ffn_w_up, ffn_w_down, shared):
    f = np.ascontiguousarray
    rows0 = np.concatenate([g_norm.reshape(16, D), c_prompt[c:c + 1], c_sample[2 * c:2 * c + 2], pool_scale.reshape(2, D)], axis=0)
    m = dict(shared)
    m.update({
        "xp": f(x_prompt[c]),
        "xs": f(x_sample[2 * c:2 * c + 2].reshape(64, D)),
        "cache": f(cache_pool[:, 2 * c:2 * c + 2].reshape(2, 30, D)),
        "rows0": f(rows0.astype(np.float32)),
    })
    return m


_CACHE = {}


def kernel(x_prompt, x_sample, cache_pool, c_prompt, c_sample, g_norm, w_ada, b_ada, sgu_w_in, sgu_g_v,
           sgu_w_s, sgu_b_s, sgu_w_out, pool_w_grp, pool_scale, ffn_w_up, ffn_w_down):
    args = [np.asarray(a, dtype=np.float32) for a in (x_prompt, x_sample, cache_pool, c_prompt, c_sample, g_norm, w_ada, b_ada,
                                                      sgu_w_in, sgu_g_v, sgu_w_s, sgu_b_s, sgu_w_out, pool_w_grp, pool_scale,
                                                      ffn_w_up, ffn_w_down)]
    (x_prompt, x_sample, cache_pool, c_prompt, c_sample, g_norm, w_ada, b_ada, sgu_w_in, sgu_g_v, sgu_w_s, sgu_b_s,
     sgu_w_out, pool_w_grp, pool_scale, ffn_w_up, ffn_w_down) = args
    B, SEQ, _ = x_prompt.shape
    n_cores = B
    ident, rcnt = make_consts()
    f = np.ascontiguousarray
    shared = {"w_ada": f(w_ada), "b_ada": f(b_ada), "w_in": f(sgu_w_in), "g_v": f(sgu_g_v), "w_s": f(sgu_w_s),
              "b_s": f(sgu_b_s.reshape(1, 2048)), "w_out": f(sgu_w_out), "pw": f(pool_w_grp), "w_up": f(ffn_w_up),
              "w_dn": f(ffn_w_down), "ident": ident, "rcnt": rcnt}
    in_maps = [core_inputs(c, *args, shared) for c in range(n_cores)]
    if SEQ not in _CACHE:
        _CACHE[SEQ] = build_program(SEQ)[0]
    nc = _CACHE[SEQ]
    res = run_bass_kernel_spmd(nc, in_maps, core_ids=list(range(n_cores)))
    R = res.results
    y_prompt = np.stack([R[c]["yp"] for c in range(n_cores)], axis=0)
    y_sample = np.concatenate([R[c]["ys"].reshape(2, 32, D) for c in range(n_cores)], axis=0)
    spp = np.stack([R[c]["spp"] for c in range(n_cores)], axis=1)
    ssg = np.concatenate([R[c]["ssg"].reshape(2, 2, 32, 2048) for c in range(n_cores)], axis=1)
    sps = np.concatenate([R[c]["sps"].reshape(2, 2, 15, D) for c in range(n_cores)], axis=1)
    return (y_prompt.astype(np.float32), y_sample.astype(np.float32), spp.astype(np.float32),
            ssg.astype(np.float32), sps.astype(np.float32))
```

```python
import numpy as np
import concourse.bass as bass
import concourse.mybir as mybir
from concourse.bass_utils import run_bass_kernel_spmd

F32 = mybir.dt.float32
BF16 = mybir.dt.bfloat16
AF = mybir.ActivationFunctionType
ALU = mybir.AluOpType

ENGS = ("pe", "act", "dve", "pool", "sp")


class Prog:
    def __init__(self, nc):
        self.nc = nc
        self.q = {e: [] for e in ENGS}
        self.esem = {e: nc.alloc_semaphore(name="sem_" + e) for e in ENGS}
        self.cnt = {e: 0 for e in ENGS}
        self.known = {e: {} for e in ENGS}
        self.keys = {}
        self.dcnt = {}
        self.semobj = {}
        for e in ENGS:
            self.semobj[self.esem[e].num] = self.esem[e]
        self.nwaits = 0

    def new_sem(self, name):
        s = self.nc.alloc_semaphore(name=name)
        self.semobj[s.num] = s
        self.dcnt[s.num] = 0
        return s

    def _deps(self, eng, reads, writes):
        need = {}
        for k in reads:
            st = self.keys.get(k)
            if st:
                for s, v in st["w"].items():
                    if need.get(s, 0) < v:
                        need[s] = v
        for k in writes:
            st = self.keys.get(k)
            if st:
                for d in (st["w"], st["r"]):
                    for s, v in d.items():
                        if need.get(s, 0) < v:
                            need[s] = v
        kn = self.known[eng]
        out = []
        for s, v in need.items():
            if eng == "pe" and s == self.esem["pe"].num:
                continue
            if kn.get(s, 0) >= v:
                continue
            kn[s] = v
            out.append((self.semobj[s], v))
        self.nwaits += len(out)
        return out

    def _commit(self, tok, reads, writes):
        s, v = tok
        for k in reads:
            st = self.keys.setdefault(k, {"w": {}, "r": {}})
            if st["r"].get(s, 0) < v:
                st["r"][s] = v
        for k in writes:
            self.keys[k] = {"w": {s: v}, "r": {}}

    def op(self, eng, fn, reads=(), writes=()):
        waits = self._deps(eng, reads, writes)
        self.cnt[eng] += 1
        tok = (self.esem[eng].num, self.cnt[eng])
        self.q[eng].append((waits, fn, (self.esem[eng], 1)))
        self._commit(tok, reads, writes)
        return tok

    def dma(self, eng, fn, sem, reads=(), writes=(), final=None):
        waits = self._deps(eng, reads, writes)
        self.dcnt[sem.num] += 16
        v = self.dcnt[sem.num] if final is None else final
        tok = (sem.num, v)
        self.q[eng].append((waits, fn, (sem, 16)))
        self._commit(tok, reads, writes)
        return tok

    def wait_keys(self, eng, keys):
        waits = self._deps(eng, (), keys)
        if waits:
            self.q[eng].append((waits, None, None))

    def emit(self):
        nc = self.nc
        with nc.Block() as block:
            def mk(name):
                def body(e):
                    for waits, fn, inc in self.q[name]:
                        for s, v in waits:
                            e.wait_ge(s, v)
                        if fn is None:
                            continue
                        ins = fn(e)
                        ins.then_inc(inc[0], inc[1])
                return body
            block.tensor(mk("pe"))
            block.scalar(mk("act"))
            block.vector(mk("dve"))
            block.gpsimd(mk("pool"))
            block.sync(mk("sp"))


D = 1024
NT = 512
NSLOT = 3
NSTREAMS = 2
SKEW = 3
LEAD0, LEADMIN, LEADMAX, FILL = 16, 4, 40, 3
DBG = {}
EPS = 1e-6
SH1, GM1, GG1, SH2, GM2, GG2 = range(6)


def build_program(SEQ, n_layers=4, do_sample=True):
    nc = bass.Bass("TRN2", target_bir_lowering=False)
    P = Prog(nc)

    def din(name, shape, dt=F32):
        return nc.dram_tensor(name, shape, dt, kind="ExternalInput").ap()

    def dout(name, shape, dt=F32):
        return nc.dram_tensor(name, shape, dt, kind="ExternalOutput").ap()

    def dscr(name, shape, dt=BF16):
        return nc.dram_tensor(name, shape, dt, kind="Internal").ap()

    xp = din("xp", [SEQ, D]); xs = din("xs", [64, D]); cache = din("cache", [2, 30, D])
    rows0 = din("rows0", [21, D])
    w_ada = din("w_ada", [4, D, 6 * D]); b_ada = din("b_ada", [4, 6 * D])
    w_in = din("w_in", [2, D, 4096]); g_v = din("g_v", [2, 2048]); w_s = din("w_s", [2, 8, 128, 128])
    b_s = din("b_s", [1, 2048]); w_out = din("w_out", [2, 2048, D]); pw = din("pw", [2, 4, 256, 256])
    w_up = din("w_up", [4, D, 4096]); w_dn = din("w_dn", [4, 4096, D])
    ident_d = din("ident", [128, 128]); rcnt_d = din("rcnt", [128, 64])
    yp = dout("yp", [SEQ, D]); ys = dout("ys", [64, D]); spp = dout("spp", [2, 15, D])
    ssg = dout("ssg", [2, 64, 2048]); sps = dout("sps", [2, 30, D])
    wb_in = dscr("wb_in", [2, 8, 128, 4096]); wb_out = dscr("wb_out", [2, 4, 128, 4096])
    wb_up = dscr("wb_up", [4, 8, 128, 4096]); wb_dn = dscr("wb_dn", [4, 8, 128, 4096])
    wb_pw = dscr("wb_pw", [2, 128, 2048])

    A = nc.alloc_sbuf_tensor
    ident = A("ident_sb", [128, 128], F32)
    onesd = A("onesd", [128, 128], BF16)
    ones2 = A("ones2", [2, 128], BF16)
    rcnt = A("rcnt_sb", [128, 4, 16], F32)
    epsb = A("epsb", [128, 1], F32)
    nhalf = A("nhalf", [128, 1], F32)
    FM0 = A("FM0", [128, 8, 21], F32)
    SC = A("SC", [128, 8, 3], F32)
    CST = A("CST", [128, 4, 6, 8, 3], F32)
    wsT = A("wsT", [128, 2, 8, 128], BF16)
    wsTs = A("wsTs", [64, 2, 8, 64], BF16)
    BI = A("BI", [2, 2, 8, 128], BF16)
    BIs = A("BIs", [2, 2, 8, 64], BF16)
    halo = A("halo", [128, 4, 8, 15], F32)
    rstd = A("rstd", [128, 1, NT], F32)
    vst = A("vst", [128, 8], F32)
    xt = A("xt", [128, 8, NT], F32)
    T1 = A("T1", [128, 8 * NT], F32)
    hbf = A("hbf", [128, 8, NT], BF16)
    BIGA = A("BIGA", [128, 8192], F32)
    BIGB = A("BIGB", [128, 12288], F32)
    WR = [A("wr%d" % i, [128, 2048], F32) for i in range(NSLOT * NSTREAMS)]
    stage = A("stage", [128, 2, D], F32)
    sq2 = A("sq2", [128, 8, NT], BF16)
    vpart = A("vpart", [128, 16], F32)
    rtmp = A("rtmp", [128, NT], F32)
    fix = A("fix", [128, 2, 16], F32)
    PS = [nc.alloc_psum_tensor("ps%d" % i, [128, 512], F32) for i in range(8)]

    def Akeys(lo, hi):
        return ["A%d" % i for i in range(lo, hi)]
    T1K = ["T1pre"]

    psi = [0]

    def next_ps():
        b = psi[0]
        psi[0] = (b + 1) % 8
        return b

    semc = [0]

    def fresh_sem():
        semc[0] += 1
        return P.new_sem("d%d" % semc[0])

    def dma1(eng, fn, reads=(), writes=()):
        return P.dma(eng, fn, fresh_sem(), reads=reads, writes=writes)

    final_keys = []

    class Ring:
        def __init__(self, rid, slot_ids):
            self.rid = rid
            self.slot_ids = slot_ids
            self.ns = len(slot_ids)
            self.blocks = []
            self.issued = 0
            self.consumed = 0
            self.sems = [P.new_sem("ring%s_%d" % (rid, i)) for i in range(self.ns)]

        def add(self, src, dt, rkeys):
            self.blocks.append((src, dt, rkeys))

        def view(self, slot, dt):
            t = WR[self.slot_ids[slot]]
            return t[:] if dt == F32 else t[:].bitcast(BF16)

        def key(self, slot):
            return "WR%d" % self.slot_ids[slot]

        def try_issue(self):
            while self.issued < len(self.blocks) and self.issued < self.consumed + self.ns:
                b = self.issued
                src, dt, rkeys = self.blocks[b]
                slot = b % self.ns
                dst = self.view(slot, dt)
                if len(src.shape) == 3:
                    dst = dst.rearrange("p (k c) -> p k c", k=src.shape[1])
                elif src.shape[-1] != dst.shape[-1]:
                    dst = dst[:, 0:src.shape[-1]]
                P.dma("sp", lambda e, dst=dst, src=src: e.dma_start(out=dst, in_=src), self.sems[slot],
                      reads=rkeys, writes=[self.key(slot)])
                self.issued += 1

    nwr = NSLOT * NSTREAMS
    rings = [Ring(str(i), list(range(i * NSLOT, (i + 1) * NSLOT))) for i in range(NSTREAMS)]
    ring_pre = Ring("pre", list(range(nwr)))
    ring_smp = Ring("smp", list(range(nwr)))

    def ring_get(st, dt):
        r = st["ring"]
        k = st["blk"]
        assert k == r.consumed, ("ring order", k, r.consumed)
        r.try_issue()
        assert k < r.issued
        assert r.blocks[k][1] == dt
        slot = k % r.ns
        return r.view(slot, dt), r.key(slot)

    def ring_done(st):
        r = st["ring"]
        r.consumed += 1
        st["blk"] += 1
        r.try_issue()

    for l in range(n_layers):
        for cb in range(24):
            ring_pre.add(w_ada[l][:, cb * 256:(cb + 1) * 256].rearrange("(k p) c -> p k c", p=128), F32, [])

    def add_tile_blocks(r):
        n0 = len(r.blocks)
        for l in range(n_layers):
            j = l // 2
            ck = ["cv%d" % l]
            if l % 2 == 0:
                for b in range(8):
                    r.add(wb_in[j, b], BF16, ck)
                r.add(g_v[j:j + 1, :].partition_broadcast(128), F32, [])
                for b in range(4):
                    r.add(wb_out[j, b], BF16, ck)
            else:
                r.add(wb_pw[j], BF16, ck)
            for b in range(8):
                r.add(wb_up[l, b], BF16, ck)
            for b in range(8):
                r.add(wb_dn[l, b], BF16, ck)
        return len(r.blocks) - n0
    BPT = 0
    for r in rings:
        for ti in range(SEQ // NT):
            BPT = add_tile_blocks(r)
    if do_sample:
        BPT = add_tile_blocks(ring_smp)

    conv = []
    for l in range(n_layers):
        j = l // 2
        lst = []
        if l % 2 == 0:
            for b in range(8):
                lst.append((wb_in[j, b].rearrange("p (k c) -> p k c", k=8),
                            w_in[j][:, b * 512:(b + 1) * 512].rearrange("(k p) c -> p k c", p=128)))
            for b in range(4):
                lst.append((wb_out[j, b].rearrange("p (k c) -> p k c", k=16),
                            w_out[j][:, b * 256:(b + 1) * 256].rearrange("(k p) c -> p k c", p=128)))
        else:
            lst.append((wb_pw[j].rearrange("p (g k d) -> p g k d", g=4, k=2),
                        pw[j].rearrange("g (k p) d -> p g k d", p=128)))
        for b in range(8):
            lst.append((wb_up[l, b].rearrange("p (k c) -> p k c", k=8),
                        w_up[l][:, b * 512:(b + 1) * 512].rearrange("(k p) c -> p k c", p=128)))
        for b in range(8):
            lst.append((wb_dn[l, b].rearrange("p (k c) -> p k c", k=32),
                        w_dn[l][:, b * 128:(b + 1) * 128].rearrange("(k p) c -> p k c", p=128)))
        conv.append(lst)

    dma1("sp", lambda e: e.dma_start(out=ident[:], in_=ident_d[:, :]), writes=["ident"])
    dma1("sp", lambda e: e.dma_start(out=rcnt[:].rearrange("p g t -> p (g t)"), in_=rcnt_d[:, :]), writes=["rcnt"])
    R0 = T1[0:21, 0:D]
    dma1("sp", lambda e: e.dma_start(out=R0, in_=rows0[:, :]), writes=T1K)
    R1 = BIGB[0:4, 0:6144]
    dma1("sp", lambda e: e.dma_start(out=R1, in_=b_ada[:, :]), writes=["BIGB"])
    WS32 = BIGA[:, 0:2048].rearrange("p (l g j) -> p l g j", l=2, g=8)
    for l in range(2):
        dma1("sp", lambda e, l=l: e.dma_start(out=WS32[:, l], in_=w_s[l].rearrange("g i j -> i g j")), writes=["WS32_%d" % l])
    WSS = BIGA[0:64, 2048:3072].rearrange("p (l g j) -> p l g j", l=2, g=8)
    P.op("pool", lambda e: e.memset(BIGA[0:64, 2048:3072], 0.0), writes=["WSS"])
    for l in range(2):
        for b in range(2):
            dma1("sp", lambda e, l=l, b=b: e.dma_start(
                out=WSS[32 * b:32 * b + 32, l, :, 32 * b:32 * b + 32],
                in_=w_s[l][:, 0:32, 0:32].rearrange("g i j -> i g j")), reads=["WSS"], writes=["WSSd%d%d" % (l, b)])
    BS32 = BIGA[0:1, 3072:5120]
    dma1("sp", lambda e: e.dma_start(out=BS32, in_=b_s[:, :]), writes=["BS32"])

    P.op("pool", lambda e: e.memset(onesd[:], 1.0 / 1024.0), writes=["onesd"])
    P.op("pool", lambda e: e.memset(ones2[:], 1.0), writes=["ones2"])
    P.op("pool", lambda e: e.memset(epsb[:], EPS), writes=["epsb"])
    P.op("pool", lambda e: e.memset(nhalf[:], -0.5), writes=["nhalf"])

    conv_sems = [P.new_sem("cv%d" % l) for l in range(n_layers)]

    def issue_conv(l):
        tot = 16 * len(conv[l])
        for (o, i) in conv[l]:
            P.q["pool"].append(([], (lambda e, o=o, i=i: e.dma_start(out=o, in_=i)), (conv_sems[l], 16)))
        P._commit((conv_sems[l].num, tot), [], ["cv%d" % l])
    issue_conv(0)

    b0 = next_ps()

    def f(e):
        for c in range(8):
            i = e.transpose(PS[b0][:, c * 21:(c + 1) * 21], R0[:, c * 128:(c + 1) * 128], ident[0:21, 0:21])
        return i
    P.op("pe", f, reads=T1K + ["ident"], writes=["ps%d" % b0])
    P.op("dve", lambda e: e.tensor_copy(out=FM0[:], in_=PS[b0][:, 0:168].rearrange("p (c r) -> p c r", c=8)),
         reads=["ps%d" % b0], writes=["FM0"])
    P.op("act", lambda e: e.activation(out=SC[:], in_=FM0[:, :, 16:19], func=AF.Silu), reads=["FM0"], writes=["SC"])
    BADA = BIGA[:, 5120:5312].rearrange("p (j l) -> p j l", j=48)
    b1 = next_ps()

    def f(e):
        for jj in range(48):
            i = e.transpose(PS[b1][:, jj * 4:(jj + 1) * 4], R1[:, jj * 128:(jj + 1) * 128], ident[0:4, 0:4])
        return i
    P.op("pe", f, reads=["BIGB", "ident"], writes=["ps%d" % b1])
    P.op("dve", lambda e: e.tensor_copy(out=BADA, in_=PS[b1][:, 0:192].rearrange("p (j l) -> p j l", j=48)),
         reads=["ps%d" % b1], writes=["BADA"])
    for l in range(2):
        for hh in range(2):
            bk = next_ps()

            def f(e, l=l, hh=hh, bk=bk):
                for g in range(4):
                    i = e.transpose(PS[bk][:, g * 128:(g + 1) * 128], WS32[:, l, hh * 4 + g, :], ident[:])
                return i
            P.op("pe", f, reads=["WS32_%d" % l, "ident"], writes=["ps%d" % bk])
            P.op("dve", lambda e, l=l, hh=hh, bk=bk: e.tensor_copy(
                out=wsT[:, l, hh * 4:(hh + 1) * 4, :], in_=PS[bk][:].rearrange("p (g i) -> p g i", g=4)),
                reads=["ps%d" % bk], writes=["wsT"])
        bk = next_ps()

        def f(e, l=l, bk=bk):
            for g in range(8):
                i = e.transpose(PS[bk][0:64, g * 64:(g + 1) * 64], WSS[:, l, g, :], ident[0:64, 0:64])
            return i
        P.op("pe", f, reads=["WSS", "WSSd%d0" % l, "WSSd%d1" % l, "ident"], writes=["ps%d" % bk])
        P.op("dve", lambda e, l=l, bk=bk: e.tensor_copy(
            out=wsTs[:, l, :, :], in_=PS[bk][0:64, :].rearrange("p (g i) -> p g i", g=8)),
            reads=["ps%d" % bk], writes=["wsTs"])
    P.op("dve", lambda e: e.memset(wsT[64:128, :, :, 0:64], 0.0), reads=["wsT"], writes=["wsT"])
    BHI = BIGA[0:1, 5312:6336].bitcast(BF16)
    BLO = BIGA[0:1, 6336:7360].bitcast(BF16)
    BH32 = BIGA[0:1, 3072 + 2048 + 2240:3072 + 2048 + 2240 + 1]
    BTMP = BIGB[0:1, 6144:8192]
    P.op("dve", lambda e: e.tensor_copy(out=BHI, in_=BS32), reads=["BS32"], writes=["BHI"])
    P.op("dve", lambda e: e.tensor_copy(out=BTMP, in_=BHI), reads=["BHI"], writes=["BTMP"])
    P.op("dve", lambda e: e.tensor_tensor(out=BLO, in0=BS32, in1=BTMP, op=ALU.subtract), reads=["BS32", "BTMP"], writes=["BLO"])
    for r, (src, key) in enumerate(((BHI, "BHI"), (BLO, "BLO"))):
        dma1("sp", lambda e, r=r, src=src: e.dma_start(out=BI[r:r + 1].rearrange("p l g i -> p (l g i)"), in_=src),
             reads=[key], writes=["BI%d" % r])
        for hh in range(2):
            dma1("sp", lambda e, r=r, src=src, hh=hh: e.dma_start(
                out=BIs[r:r + 1, :, :, hh * 32:(hh + 1) * 32],
                in_=src.rearrange("p (l g i) -> p l g i", l=2, g=8)[:, :, :, 0:32]),
                reads=[key], writes=["BIs%d%d" % (r, hh)])
    BIK = ["BI0", "BI1"]
    BISK = ["BIs00", "BIs01", "BIs10", "BIs11"]

    evi = [0]

    def evac_copy(dst, src, reads, writes):
        evi[0] ^= 1
        if evi[0]:
            P.op("act", lambda e: e.activation(out=dst, in_=src, func=AF.Copy), reads=reads, writes=writes)
        else:
            P.op("dve", lambda e: e.tensor_copy(out=dst, in_=src), reads=reads, writes=writes)

    MOD = BIGA[:, 7360:7936].rearrange("p (l j t) -> p l j t", l=4, j=48)
    pre_st = {"blk": 0, "ring": ring_pre}
    MODT = BIGB[0:3, 6144:12288]
    MODTK = ["MODT%d" % q for q in range(12)]
    for l in range(n_layers):
        for cb2 in range(12):
            bkt = next_ps()
            for half in range(2):
                wv, wk = ring_get(pre_st, F32)
                wv3 = wv.rearrange("p (k c) -> p k c", k=8)

                def f(e, wv3=wv3, half=half, bkt=bkt):
                    for k in range(8):
                        i = e.matmul(PS[bkt][0:3, half * 256:(half + 1) * 256], lhsT=SC[:, k, :], rhs=wv3[:, k, :],
                                     start=(k == 0), stop=(k == 7))
                    return i
                P.op("pe", f, reads=[wk, "SC"], writes=["ps%d" % bkt] if half == 0 else [])
                if half:
                    P._commit((P.esem["pe"].num, P.cnt["pe"]), [], ["ps%d" % bkt])
                ring_done(pre_st)
            evac_copy(MODT[:, cb2 * 512:(cb2 + 1) * 512], PS[bkt][0:3, :], ["ps%d" % bkt], [MODTK[cb2], "BTMP"])
        bk = next_ps()

        def f(e, bk=bk):
            for jx in range(48):
                i = e.transpose(PS[bk][:, jx * 3:(jx + 1) * 3], MODT[:, jx * 128:(jx + 1) * 128], ident[0:3, 0:3])
            return i
        P.op("pe", f, reads=MODTK + ["ident"], writes=["ps%d" % bk])
        P.op("dve", lambda e, l=l, bk=bk: e.tensor_tensor(
            out=MOD[:, l], in0=PS[bk][:, 0:144].rearrange("p (j t) -> p j t", j=48),
            in1=BADA[:, :, l:l + 1].to_broadcast([128, 48, 3]), op=ALU.add),
            reads=["ps%d" % bk, "BADA"], writes=["MOD%d" % l])

        def gb(k, l=l):
            return FM0[:, :, 4 * l + k:4 * l + k + 1].to_broadcast([128, 8, 3])
        P.op("dve", lambda e, l=l: e.tensor_copy(out=CST[:, l, SH1], in_=MOD[:, l, 0:8, :]), reads=["MOD%d" % l], writes=["CST"])
        P.op("dve", lambda e, l=l, gb=gb: e.scalar_tensor_tensor(out=CST[:, l, GM1], in0=MOD[:, l, 8:16, :], scalar=1.0, in1=gb(0),
                                                                   op0=ALU.add, op1=ALU.mult), reads=["MOD%d" % l, "FM0"], writes=["CST"])
        P.op("dve", lambda e, l=l, gb=gb: e.tensor_tensor(out=CST[:, l, GG1], in0=MOD[:, l, 16:24, :], in1=gb(1), op=ALU.mult),
             reads=["MOD%d" % l, "FM0"], writes=["CST"])
        P.op("dve", lambda e, l=l: e.tensor_copy(out=CST[:, l, SH2], in_=MOD[:, l, 24:32, :]), reads=["MOD%d" % l], writes=["CST"])
        P.op("dve", lambda e, l=l, gb=gb: e.scalar_tensor_tensor(out=CST[:, l, GM2], in0=MOD[:, l, 32:40, :], scalar=1.0, in1=gb(2),
                                                                   op0=ALU.add, op1=ALU.mult), reads=["MOD%d" % l, "FM0"], writes=["CST"])
        P.op("dve", lambda e, l=l, gb=gb: e.tensor_tensor(out=CST[:, l, GG2], in0=MOD[:, l, 40:48, :], in1=gb(3), op=ALU.mult),
             reads=["MOD%d" % l, "FM0"], writes=["CST"])

    P.wait_keys("pool", ["MOD%d" % (n_layers - 1)])
    for l in range(1, n_layers):
        issue_conv(l)
    PRE_KEYS = ["BIGB", "WS32_0", "WS32_1", "WSS", "WSSd00", "WSSd01", "WSSd10", "WSSd11", "BS32", "BHI", "BLO", "BTMP",
                "BADA"] + ["MOD%d" % l for l in range(n_layers)] + MODTK
    for eng in ("act", "dve", "pool", "pe"):
        P.wait_keys(eng, PRE_KEYS + T1K)

    CH_ENG = ["dve"] * 8
    gpi = [0]

    def next_ps():
        b = gpi[0]
        gpi[0] = (b + 1) % 6
        return b

    def mk_stream(name, kind, c0, n, s0, NS, TS, sidx, segs, t1off, vbfoff, a0, bb0, slots):
        st = dict(name=name, kind=kind, c0=c0, n=n, s0=s0, NS=NS, TS=TS, sidx=sidx, segs=segs, slots=slots, rb=0, sb=6 + sidx,
                  ring=(ring_smp if kind == "s" else rings[sidx if NSTREAMS > 1 else 0]))
        st["junk"] = rtmp[:, c0:c0 + n].bitcast(BF16) if n == 256 else rtmp[:].bitcast(BF16)[:, 0:512]
        st["T1v"] = T1[:, t1off:t1off + 8 * n].rearrange("p (c n) -> p c n", c=8)
        st["vbf"] = T1[:, vbfoff:vbfoff + NS * 1024].bitcast(BF16).rearrange("p (s n) -> p s n", s=NS)
        st["u32"] = BIGA[:, a0:a0 + 16 * n].rearrange("p (c n) -> p c n", c=16)
        st["abf"] = BIGA[:, a0:a0 + 16 * n].bitcast(BF16).rearrange("p (c n) -> p c n", c=32)
        st["vbig"] = BIGB[:, bb0:bb0 + NS * 2048].rearrange("p (s n) -> p s n", s=NS)
        st["usb"] = BIGB[:, bb0 + NS * 2048:bb0 + NS * 2048 + 8 * n].bitcast(BF16).rearrange("p (c n) -> p c n", c=16)
        E = 94 if kind == "s" else n + 15
        st["E"] = E
        st["hp"] = BIGB[:, bb0:bb0 + 8 * E].rearrange("p (c e) -> p c e", c=8)
        st["W1"] = BIGB[:, bb0 + 8 * E:bb0 + 16 * E].rearrange("p (c e) -> p c e", c=8)
        st["W2"] = BIGB[:, bb0 + 16 * E:bb0 + 22 * E].rearrange("p (c e) -> p c e", c=6)
        for nm in ("T1", "x", "h", "q", "hp"):
            st[nm + "K"] = ["%s%s_%d" % (nm, name, c) for c in range(8)]
        st["AK"] = ["A%s_%d" % (name, i) for i in range(32)]
        st["USK"] = ["us%s_%d" % (name, i) for i in range(16)]
        st["bbkeys"] = set()
        return st

    def X(st, c):
        return xt[:, c, st["c0"]:st["c0"] + st["n"]]

    def SQ(st, c):
        return sq2[:, c, st["c0"]:st["c0"] + st["n"]]

    def H(st, c, a=0, w=None):
        w = st["n"] - a if w is None else w
        return hbf[:, c, st["c0"] + a:st["c0"] + a + w]

    def stats_mm(st, c, first, last):
        if not last:
            return
        n, sb = st["n"], st["sb"]

        def f(e):
            for cc in range(8):
                i = e.matmul(PS[sb][:, 0:n], lhsT=onesd[:], rhs=SQ(st, cc), start=(cc == 0), stop=(cc == 7))
            return i
        P.op("pe", f, reads=st["qK"] + ["onesd"], writes=["ps%d" % sb])

    def finish_rstd(st):
        n, sb, rb = st["n"], st["sb"], st["rb"]
        rk = "rstd%s%d" % (st["name"], rb)
        ap = rstd[:, rb, st["c0"]:st["c0"] + n]
        if DBG.get("lnexp"):
            P.op("act", lambda e: e.activation(out=ap, in_=PS[sb][:, 0:n], func=AF.Ln, bias=epsb[:, 0:1], scale=1.0),
                 reads=["ps%d" % sb, "epsb"], writes=[rk])
            P.op("act", lambda e: e.activation(out=ap, in_=ap, func=AF.Exp, scale=-0.5), reads=[rk], writes=[rk])
            return ap, rk
        P.op("act", lambda e: e.activation(out=ap, in_=PS[sb][:, 0:n], func=AF.Sqrt, bias=epsb[:, 0:1], scale=1.0),
             reads=["ps%d" % sb, "epsb"], writes=[rk])
        P.op("dve", lambda e: e.reciprocal(out=ap, in_=ap), reads=[rk], writes=[rk])
        return ap, rk

    def x_squares(st):
        for c in range(8):
            P.op("act", lambda e, c=c: e.activation(out=SQ(st, c), in_=X(st, c), func=AF.Square), reads=[st["xK"][c]], writes=[st["qK"][c]])

    def pre_norm(st, l, kg, ks, dst_fn, dkeys):
        rs, rk = finish_rstd(st)
        T1v = st["T1v"]
        for c in range(8):
            P.op(CH_ENG[c], lambda e, c=c: e.tensor_tensor(out=T1v[:, c, :], in0=X(st, c), in1=rs, op=ALU.mult),
                 reads=[st["xK"][c], rk], writes=[st["T1K"][c]])
            for (a, w, t) in st["segs"]:
                P.op("act", lambda e, c=c, a=a, w=w, t=t: e.activation(
                    out=dst_fn(c, (a, w, t)), in_=T1v[:, c, a:a + w], func=AF.Identity,
                    scale=CST[:, l, kg, c, t:t + 1], bias=CST[:, l, ks, c, t:t + 1]),
                    reads=[st["T1K"][c], "CST"], writes=[dkeys[c]])

    def evac_mix(st, l, kgg, bk, m, sc=None):
        n = st["n"]
        T1v = st["T1v"]
        if sc is None:
            P.op("act", lambda e: e.activation(out=SQ(st, m), in_=PS[bk][:, 0:n], func=AF.Square),
                 reads=["ps%d" % bk], writes=[st["qK"][m]])
        else:
            P.op("act", lambda e: e.activation(out=SQ(st, m), in_=PS[bk][:, 0:n], func=AF.Square, scale=sc),
                 reads=["ps%d" % bk, "FM0"], writes=[st["qK"][m]])
        for (a, w, t) in st["segs"]:
            if sc is None:
                P.op("dve", lambda e, a=a, w=w, t=t: e.tensor_scalar(out=T1v[:, m, a:a + w], in0=PS[bk][:, a:a + w],
                                                                     scalar1=CST[:, l, kgg, m, t:t + 1], scalar2=None, op0=ALU.mult),
                     reads=["ps%d" % bk, "CST", st["qK"][m]], writes=[st["T1K"][m]])
            else:
                P.op("dve", lambda e, a=a, w=w, t=t: e.tensor_scalar(out=T1v[:, m, a:a + w], in0=PS[bk][:, a:a + w],
                                                                     scalar1=sc, scalar2=CST[:, l, kgg, m, t:t + 1],
                                                                     op0=ALU.mult, op1=ALU.mult),
                     reads=["ps%d" % bk, "CST", "FM0", st["qK"][m]], writes=[st["T1K"][m]])

    def post_norm(st, l, want_sq):
        rs, rk = finish_rstd(st)
        T1v = st["T1v"]
        for c in range(8):
            E = CH_ENG[c]
            P.op(E, lambda e, c=c: e.tensor_tensor(out=T1v[:, c, :], in0=T1v[:, c, :], in1=rs, op=ALU.mult),
                 reads=[st["T1K"][c], rk], writes=[st["T1K"][c]])
            P.op(E, lambda e, c=c: e.tensor_tensor(out=X(st, c), in0=X(st, c), in1=T1v[:, c, :], op=ALU.add),
                 reads=[st["T1K"][c], st["xK"][c]], writes=[st["xK"][c]])
            if want_sq:
                P.op("act", lambda e, c=c: e.activation(out=SQ(st, c), in_=X(st, c), func=AF.Square),
                     reads=[st["xK"][c]], writes=[st["qK"][c]])

    def ffn_gen(st, l, last_layer):
        n, c0 = st["n"], st["c0"]
        abf = st["abf"]
        pre_norm(st, l, GM2, SH2, lambda c, sg: H(st, c, sg[0], sg[1]), st["hK"])
        yield "chain"
        for b in range(8):
            wv, wk = ring_get(st, BF16)
            for jj in range(4):
                j = b * 4 + jj
                bk = next_ps()

                def f(e, wv=wv, jj=jj, bk=bk):
                    for k in range(8):
                        i = e.matmul(PS[bk][:, 0:n], lhsT=wv[:, k * 512 + jj * 128:k * 512 + (jj + 1) * 128],
                                     rhs=H(st, k), start=(k == 0), stop=(k == 7))
                    return i
                P.op("pe", f, reads=[wk] + st["hK"], writes=["ps%d" % bk])
                rk = "rtmp%s" % st["name"]
                rt = rtmp[:, c0:c0 + n]
                if False:
                    P.op("dve", lambda e, j=j, bk=bk: e.scalar_tensor_tensor(out=abf[:, j, :], in0=PS[bk][:, 0:n], scalar=0.0,
                                                                             in1=PS[bk][:, 0:n], op0=ALU.max, op1=ALU.mult),
                         reads=["ps%d" % bk], writes=[st["AK"][j]])
                    continue
                P.op("act", lambda e, bk=bk, rt=rt: e.activation(out=rt, in_=PS[bk][:, 0:n], func=AF.Relu),
                     reads=["ps%d" % bk], writes=[rk])
                if DBG.get("relupool"):
                    P.op("pool", lambda e, j=j, rt=rt: e.tensor_tensor(out=abf[:, j, :], in0=rt, in1=rt, op=ALU.mult),
                         reads=[rk], writes=[st["AK"][j]])
                else:
                    P.op("act", lambda e, j=j, rt=rt: e.activation(out=abf[:, j, :], in_=rt, func=AF.Square),
                         reads=[rk], writes=[st["AK"][j]])
            ring_done(st)
            yield "blk"
        pend = None
        for m in range(8):
            wv, wk = ring_get(st, BF16)
            bk = next_ps()

            def f(e, wv=wv, bk=bk):
                for k in range(32):
                    i = e.matmul(PS[bk][:, 0:n], lhsT=wv[:, k * 128:(k + 1) * 128], rhs=abf[:, k, :],
                                 start=(k == 0), stop=(k == 31))
                return i
            P.op("pe", f, reads=[wk] + st["AK"], writes=["ps%d" % bk])
            evac_mix(st, l, GG2, bk, m)
            if pend is not None:
                stats_mm(st, pend, pend == 0, False)
            pend = m
            ring_done(st)
            yield "blk"
        stats_mm(st, 7, False, True)
        post_norm(st, l, not last_layer)
        if not last_layer:
            yield "chain"
            stats_mm(st, 7, False, True)

    def sgu_gen(st, l):
        j = l // 2
        n, c0, TS, NS, s0 = st["n"], st["c0"], st["TS"], st["NS"], st["s0"]
        smp = st["kind"] == "s"
        u32, vbig, vbf, usb = st["u32"], st["vbig"], st["vbf"], st["usb"]
        nm = st["name"]
        pre_norm(st, l, GM1, SH1, lambda c, sg: H(st, c, sg[0], sg[1]), st["hK"])
        yield "chain"
        for b in range(4):
            wv, wk = ring_get(st, BF16)
            for jj in range(4):
                ju = b * 4 + jj
                bk = next_ps()

                def f(e, wv=wv, jj=jj, bk=bk):
                    for k in range(8):
                        i = e.matmul(PS[bk][:, 0:n], lhsT=wv[:, k * 512 + jj * 128:k * 512 + (jj + 1) * 128],
                                     rhs=H(st, k), start=(k == 0), stop=(k == 7))
                    return i
                P.op("pe", f, reads=[wk] + st["hK"], writes=["ps%d" % bk])
                P.op("act", lambda e, ju=ju, bk=bk: e.activation(out=u32[:, ju, :], in_=PS[bk][:, 0:n], func=AF.Gelu_apprx_tanh),
                     reads=["ps%d" % bk], writes=st["AK"][2 * ju:2 * ju + 2])
            ring_done(st)
            yield "blk"
        for nb in range(4):
            wv, wk = ring_get(st, BF16)
            for i in range(NS):
                s = s0 + i
                bk = next_ps()

                def f(e, wv=wv, i=i, bk=bk):
                    for k in range(8):
                        ins = e.matmul(PS[bk][0:TS, :], lhsT=H(st, k, i * TS, TS), rhs=wv[:, k * 512:(k + 1) * 512],
                                       start=(k == 0), stop=(k == 7))
                    return ins
                P.op("pe", f, reads=[wk] + st["hK"], writes=["ps%d" % bk])
                vk = "vb%d_%d" % (s, nb)
                st["bbkeys"].add(vk)
                P.op("act", lambda e, i=i, nb=nb, bk=bk: e.activation(out=vbig[0:TS, i, nb * 512:(nb + 1) * 512], in_=PS[bk][0:TS, :],
                                                                      func=AF.Gelu_apprx_tanh),
                     reads=["ps%d" % bk], writes=[vk])
                P.op("act", lambda e, i=i, nb=nb, s=s: e.activation(out=st['junk'][0:TS, :], in_=vbig[0:TS, i, nb * 512:(nb + 1) * 512],
                                                                    func=AF.Square, accum_out=vpart[0:TS, s * 4 + nb:s * 4 + nb + 1]),
                     reads=[vk], writes=["vp%d_%d" % (s, nb)])
            ring_done(st)
            yield "blk"
        VPK = ["vp%d_%d" % (s0 + i, q) for i in range(NS) for q in range(4)]
        vsk = "vst" + nm
        P.op("dve", lambda e: e.reduce_sum(out=vst[0:TS, s0:s0 + NS],
                                           in_=vpart[0:TS, s0 * 4:(s0 + NS) * 4].rearrange("p (s q) -> p s q", q=4),
                                           axis=mybir.AxisListType.X), reads=VPK, writes=[vsk])
        P.op("dve", lambda e: e.tensor_scalar(out=vst[0:TS, 4 + s0:4 + s0 + NS], in0=vst[0:TS, s0:s0 + NS], scalar1=1.0 / 2048.0,
                                              scalar2=EPS, op0=ALU.mult, op1=ALU.add), reads=[vsk], writes=["vrs" + nm])
        P.op("pool", lambda e: e.tensor_tensor(out=vst[0:TS, 4 + s0:4 + s0 + NS], in0=vst[0:TS, 4 + s0:4 + s0 + NS],
                                               in1=nhalf[0:TS, 0:1].to_broadcast([TS, NS]), op=ALU.pow),
             reads=["vrs" + nm, "nhalf"], writes=["vrs" + nm])
        gv, gk = ring_get(st, F32)
        for i in range(NS):
            s = s0 + i
            rd = ["vb%d_%d" % (s, q) for q in range(4)] + ["vrs" + nm, gk]
            if smp:
                vstate = stage[0:64].rearrange("p a d -> p (a d)")
                P.op("dve", lambda e: e.scalar_tensor_tensor(out=vstate, in0=vbig[0:64, 0, :], scalar=vst[0:64, 4:5], in1=gv[0:64, :],
                                                             op0=ALU.mult, op1=ALU.mult), reads=rd, writes=["stage0", "stage1"])
                dma1("sp", lambda e: e.dma_start(out=ssg[j], in_=vstate), reads=["stage0", "stage1"])
                final_keys.extend(["stage0", "stage1"])
                P.op("pool", lambda e: e.tensor_copy(out=vbf[0:64, 0, :], in_=vstate), reads=["stage0", "stage1"], writes=st["T1K"])
            else:
                P.op("dve", lambda e, i=i, s=s: e.scalar_tensor_tensor(out=vbf[0:TS, i, :], in0=vbig[0:TS, i, :],
                                                                       scalar=vst[0:TS, 4 + s:5 + s], in1=gv[0:TS, :],
                                                                       op0=ALU.mult, op1=ALU.mult), reads=rd, writes=st["T1K"])
        ring_done(st)
        yield "blk"
        yield "chain"
        for dj in range(16):
            g = dj // 2
            bk = next_ps()

            def f(e, dj=dj, g=g, bk=bk):
                for i in range(NS):
                    o = PS[bk][:, i * TS:(i + 1) * TS]
                    e.matmul(o, lhsT=vbf[0:TS, i, dj * 128:(dj + 1) * 128],
                             rhs=(wsTs[0:64, j, g, :] if smp else wsT[:, j, g, :]), start=True, stop=False)
                    ins = e.matmul(o, lhsT=ones2[:, :], rhs=(BIs[:, j, g, :] if smp else BI[:, j, g, :]), start=False, stop=True)
                return ins
            P.op("pe", f, reads=st["T1K"] + ["wsT", "wsTs", "ones2"] + BIK + BISK, writes=["ps%d" % bk])
            st["bbkeys"].add(st["USK"][dj])
            P.op("dve", lambda e, dj=dj, bk=bk: e.tensor_tensor(out=usb[:, dj, :], in0=PS[bk][:, 0:n], in1=u32[:, dj, :], op=ALU.mult),
                 reads=["ps%d" % bk] + st["AK"][2 * dj:2 * dj + 2], writes=[st["USK"][dj]])
        pend = None
        for b in range(4):
            wv, wk = ring_get(st, BF16)
            for mm in range(2):
                m = b * 2 + mm
                bk = next_ps()

                def f(e, wv=wv, mm=mm, bk=bk):
                    for k in range(16):
                        ins = e.matmul(PS[bk][:, 0:n], lhsT=wv[:, k * 256 + mm * 128:k * 256 + (mm + 1) * 128], rhs=usb[:, k, :],
                                       start=(k == 0), stop=(k == 15))
                    return ins
                P.op("pe", f, reads=[wk] + st["USK"], writes=["ps%d" % bk])
                evac_mix(st, l, GG1, bk, m)
                if pend is not None:
                    stats_mm(st, pend, pend == 0, False)
                pend = m
            ring_done(st)
            yield "blk"
        stats_mm(st, 7, False, True)
        post_norm(st, l, True)
        yield "chain"
        stats_mm(st, 7, False, True)

    def pool_gen(st, tl, l):
        j = l // 2
        n, c0, E = st["n"], st["c0"], st["E"]
        smp = st["kind"] == "s"
        nm = st["name"]
        nseg, L = (2, 32) if smp else (1, n)
        SL = L + 15
        hp, W1, W2 = st["hp"], st["W1"], st["W2"]
        hp4 = hp.rearrange("p c (b e) -> p c b e", b=nseg)
        hk = "hphalo" + nm
        for k_ in st["hpK"] + [hk] + ["W%d%sg%d" % (a, nm, g) for a in (1, 2) for g in range(4)]:
            st["bbkeys"].add(k_)

        def val(ap3, ca, cb):
            return ap3[:, ca:cb, :].rearrange("p c (b e) -> p c b e", b=nseg)[:, :, :, 15:SL]
        if smp:
            stg = stage[0:30, 1, :]
            dma1("sp", lambda e: e.dma_start(out=stg, in_=cache[j]), writes=["stage1"])
            bk = next_ps()

            def f(e):
                for c in range(8):
                    i = e.transpose(PS[bk][:, c * 30:(c + 1) * 30], stg[:, c * 128:(c + 1) * 128], ident[0:30, 0:30])
                return i
            P.op("pe", f, reads=["stage1", "ident"], writes=["ps%d" % bk])
            P.op("dve", lambda e: e.tensor_copy(out=hp4[:, :, :, 0:15],
                                                in_=PS[bk][:, 0:240].rearrange("p (c b r) -> p c b r", c=8, b=2)),
                 reads=["ps%d" % bk], writes=[hk])
        elif c0 == 0 and tl["first"]:
            P.op("pool", lambda e: e.memset(hp[:, :, 0:15], 0.0), writes=[hk])
        else:
            hsrc = (0 if c0 > 0 else 2) + j
            P.op("pool", lambda e: e.tensor_copy(out=hp[:, :, 0:15], in_=halo[:, hsrc]), reads=["halo%d" % hsrc], writes=[hk])

        def dst_fn(c, sg):
            a, w, t = sg
            if smp:
                return hp4[:, c, t - 1, 15:SL]
            return hp[:, c, 15 + a:15 + a + w]
        pre_norm(st, l, GM1, SH1, dst_fn, st["hpK"])
        if not smp:
            hdst = (2 if c0 + n == NT else 0) + j
            P.op("pool", lambda e: e.tensor_copy(out=halo[:, hdst], in_=hp[:, :, n:n + 15]), reads=st["hpK"], writes=["halo%d" % hdst])
        if smp or (tl["last"] and c0 + n == NT):
            for b in range(nseg):
                bk = next_ps()
                bk2 = next_ps()
                slot = b if smp else st["slots"][0]

                def f(e, b=b, bk=bk, bk2=bk2):
                    for c in range(8):
                        o = (PS[bk] if c < 4 else PS[bk2])[0:15, (c % 4) * 128:(c % 4 + 1) * 128]
                        i = e.transpose(o, hp4[:, c, b, L:SL], ident[:])
                    return i
                P.op("pe", f, reads=st["hpK"] + ["ident"], writes=["ps%d" % bk, "ps%d" % bk2])
                sk = "stage%d" % slot
                P.op("dve", lambda e, slot=slot, bk=bk: e.tensor_copy(out=stage[0:15, slot, 0:512], in_=PS[bk][0:15, :]),
                     reads=["ps%d" % bk], writes=[sk])
                P.op("dve", lambda e, slot=slot, bk2=bk2: e.tensor_copy(out=stage[0:15, slot, 512:1024], in_=PS[bk2][0:15, :]),
                     reads=["ps%d" % bk2, sk], writes=[sk])
                dsto = sps[j, b * 15:(b + 1) * 15, :] if smp else spp[j]
                dma1("sp", lambda e, slot=slot, dsto=dsto: e.dma_start(out=dsto, in_=stage[0:15, slot, :]), reads=[sk])
                final_keys.append(sk)
        RD = st["hpK"] + [hk]
        pv = hbf[:, :, c0:c0 + n].rearrange("p c (b e) -> p c b e", b=nseg)
        for g in (3, 2, 1, 0):
            E_ = "dve" if g in (3, 0) else "pool"
            ca, cb = 2 * g, 2 * g + 2
            k1, k2 = "W1%sg%d" % (nm, g), "W2%sg%d" % (nm, g)
            P.op(E_, lambda e, ca=ca, cb=cb: e.tensor_tensor(out=W1[:, ca:cb, 1:E], in0=hp[:, ca:cb, 1:E], in1=hp[:, ca:cb, 0:E - 1], op=ALU.add),
                 reads=RD, writes=[k1])
            src, srck, off = W1, k1, 0
            if g >= 1:
                P.op(E_, lambda e, ca=ca, cb=cb: e.tensor_tensor(out=W2[:, ca - 2:cb - 2, 3:E], in0=W1[:, ca:cb, 3:E], in1=W1[:, ca:cb, 1:E - 2],
                                                                 op=ALU.add), reads=[k1], writes=[k2])
                src, srck, off = W2, k2, 2
            if g >= 2:
                P.op(E_, lambda e, ca=ca, cb=cb: e.tensor_tensor(out=W1[:, ca:cb, 7:E], in0=W2[:, ca - 2:cb - 2, 7:E], in1=W2[:, ca - 2:cb - 2, 3:E - 4],
                                                                 op=ALU.add), reads=[k2, k1], writes=[k1])
                src, srck, off = W1, k1, 0
            if g >= 3:
                P.op(E_, lambda e, ca=ca, cb=cb: e.tensor_tensor(out=W2[:, ca - 2:cb - 2, 15:E], in0=W1[:, ca:cb, 15:E], in1=W1[:, ca:cb, 7:E - 8],
                                                                 op=ALU.add), reads=[k1, k2], writes=[k2])
                src, srck, off = W2, k2, 2
            P.op("dve", lambda e, g=g, src=src, off=off, ca=ca, cb=cb: e.scalar_tensor_tensor(
                out=pv[:, ca:cb], in0=val(src, ca - off, cb - off), scalar=1.0 / (2 << g), in1=val(hp, ca, cb),
                op0=ALU.mult, op1=ALU.subtract), reads=[srck] + RD, writes=st["hK"][ca:cb])
            if c0 == 0 and tl["first"] and not smp:
                P.op("dve", lambda e, g=g, src=src, off=off, ca=ca, cb=cb: e.tensor_tensor(
                    out=fix[:], in0=src[:, ca - off:cb - off, 15:31], in1=rcnt[:, g:g + 1, :].to_broadcast([128, 2, 16]), op=ALU.mult),
                    reads=[srck, "rcnt"], writes=["fix"])
                P.op("dve", lambda e, ca=ca, cb=cb: e.tensor_tensor(out=hbf[:, ca:cb, 0:16], in0=fix[:], in1=hp[:, ca:cb, 15:31],
                                                                    op=ALU.subtract),
                     reads=["fix"] + RD, writes=st["hK"][ca:cb])
        yield "chain"
        wv, wk = ring_get(st, BF16)
        pend = None
        for m in range(8):
            g, mm = m // 2, m % 2
            bk = next_ps()

            def f(e, g=g, mm=mm, bk=bk):
                for kc in range(2):
                    i = e.matmul(PS[bk][:, 0:n], lhsT=wv[:, (g * 2 + kc) * 256 + mm * 128:(g * 2 + kc) * 256 + (mm + 1) * 128],
                                 rhs=H(st, 2 * g + kc), start=(kc == 0), stop=(kc == 1))
                return i
            P.op("pe", f, reads=[wk, st["hK"][2 * g], st["hK"][2 * g + 1]], writes=["ps%d" % bk])
            evac_mix(st, l, GG1, bk, m, sc=FM0[:, m, 19 + j:20 + j])
            if pend is not None:
                stats_mm(st, pend, pend == 0, False)
            pend = m
        stats_mm(st, 7, False, True)
        ring_done(st)
        yield "blk"
        post_norm(st, l, True)
        yield "chain"
        stats_mm(st, 7, False, True)

    def load_stream(st, tl):
        n, c0, TS, NS, s0 = st["n"], st["c0"], st["TS"], st["NS"], st["s0"]
        for i in range(NS):
            slot = st["slots"][i % len(st["slots"])]
            sk = "stage%d" % slot
            r0 = tl["row0"] + (s0 + i) * 128
            src = xs[:, :] if st["kind"] == "s" else xp[r0:r0 + 128, :]
            P.dma("sp", lambda e, slot=slot, src=src: e.dma_start(out=stage[0:TS, slot, :], in_=src), stg_sems[slot], writes=[sk])
            yield "chain"
            for hh in range(2):
                bk = next_ps()

                def f(e, slot=slot, hh=hh, bk=bk):
                    for c in range(4):
                        ins = e.transpose(PS[bk][:, c * TS:(c + 1) * TS], stage[0:TS, slot, (hh * 4 + c) * 128:(hh * 4 + c + 1) * 128],
                                          ident[0:TS, 0:TS])
                    return ins
                P.op("pe", f, reads=[sk, "ident"], writes=["ps%d" % bk])
                evac_copy(xt[:, hh * 4:(hh + 1) * 4, c0 + i * TS:c0 + (i + 1) * TS], PS[bk][:, 0:4 * TS].rearrange("p (c t) -> p c t", c=4),
                          ["ps%d" % bk], st["xK"][hh * 4:(hh + 1) * 4])
        x_squares(st)

    def store_stream(st, tl):
        n, c0, TS, NS, s0 = st["n"], st["c0"], st["TS"], st["NS"], st["s0"]
        for i in range(NS):
            slot = st["slots"][i % len(st["slots"])]
            sk = "stage%d" % slot
            for hh in range(2):
                bk = next_ps()

                def f(e, i=i, hh=hh, bk=bk):
                    for c in range(4):
                        ins = e.transpose(PS[bk][0:TS, c * 128:(c + 1) * 128], xt[:, hh * 4 + c, c0 + i * TS:c0 + (i + 1) * TS], ident[:])
                    return ins
                P.op("pe", f, reads=st["xK"][hh * 4:(hh + 1) * 4] + ["ident"], writes=["ps%d" % bk])
                evac_copy(stage[0:TS, slot, hh * 512:(hh + 1) * 512], PS[bk][0:TS, :], ["ps%d" % bk] + ([sk] if hh else []), [sk])
            r0 = tl["row0"] + (s0 + i) * 128
            dst = ys[:, :] if st["kind"] == "s" else yp[r0:r0 + 128, :]
            P.dma("sp", lambda e, slot=slot, dst=dst: e.dma_start(out=dst, in_=stage[0:TS, slot, :]), stg_sems[slot], reads=[sk])
            final_keys.append(sk)

    evi = [0]

    def evac_copy(dst, src, reads, writes):
        evi[0] ^= 1
        if evi[0]:
            P.op("act", lambda e: e.activation(out=dst, in_=src, func=AF.Copy), reads=reads, writes=writes)
        else:
            P.op("dve", lambda e: e.tensor_copy(out=dst, in_=src), reads=reads, writes=writes)

    stg_sems = [P.new_sem("stg%d" % i) for i in range(2)]

    def stream_gen(st, tls, blk0_fn):
        for ti, tl in enumerate(tls):
            st["blk"] = blk0_fn(ti)
            yield from load_stream(st, tl)
            yield "chain"
            stats_mm(st, 7, False, True)
            for l in range(n_layers):
                for eng in ("act", "dve", "pool"):
                    P.wait_keys(eng, sorted(st["bbkeys"]))
                if l % 2 == 0:
                    yield from sgu_gen(st, l)
                else:
                    yield from pool_gen(st, tl, l)
                yield from ffn_gen(st, l, l == n_layers - 1)
            yield "chain"
            store_stream(st, tl)

    ptiles = [dict(row0=t * NT, first=(t == 0), last=(t == SEQ // NT - 1)) for t in range(SEQ // NT)]
    ADA = n_layers * 24
    if NSTREAMS == 1:
        sts = [mk_stream("P", "p", 0, NT, 0, 4, 128, 0, [(0, NT, 0)], 0, 0, 0, 0, [0, 1])]
    else:
        sts = [mk_stream("L", "p", 0, 256, 0, 2, 128, 0, [(0, 256, 0)], 0, 0, 0, 0, [0]),
               mk_stream("G", "p", 256, 256, 2, 2, 128, 1, [(0, 256, 0)], 2048, 2048, 4096, 6144, [1])]
    gens = [stream_gen(s_, ptiles, (lambda ti: ti * BPT)) for s_ in sts]

    def adv(g, k):
        for _ in range(k):
            try:
                next(g)
            except StopIteration:
                return True
        return False
    if len(gens) == 1:
        adv(gens[0], 1 << 30)
    else:
        cnt = [0, 0]
        done = [False, False]

        def allowed(i):
            if done[i]:
                return False
            if done[i ^ 1]:
                return True
            lead = cnt[0] - cnt[1]
            return lead < LEADMAX if i == 0 else lead > LEADMIN
        cur = 0
        started = False
        budget = None
        while not (done[0] and done[1]):
            if not started and (cnt[0] >= LEAD0 or done[0]):
                started = True
            if not allowed(cur):
                cur ^= 1
                budget = None
                if not allowed(cur):
                    raise RuntimeError("scheduler stuck")
            try:
                tag = next(gens[cur])
            except StopIteration:
                done[cur] = True
                cur ^= 1
                budget = None
                continue
            if tag == "blk":
                cnt[cur] += 1
                if budget is not None:
                    budget -= 1
            if not started:
                continue
            o = cur ^ 1
            if tag == "chain":
                if allowed(o):
                    cur = o
                    budget = FILL
            elif budget is not None and budget <= 0 and allowed(o):
                cur = o
                budget = None
    if do_sample:
        ss = mk_stream("S", "s", 0, 64, 0, 1, 64, 0, [(0, 32, 1), (32, 32, 2)], 2048, 0, 0, 0, [0])
        adv(stream_gen(ss, [dict(row0=0, first=False, last=False)], lambda ti: 0), 1 << 30)

    P.wait_keys("sp", list(dict.fromkeys(final_keys)))
    for eng in ("act", "dve", "pool", "pe"):
        P.wait_keys(eng, [k for s_ in sts for k in s_["xK"]])
    P.emit()
    return nc, P


def make_consts():
    ident = np.eye(128, dtype=np.float32)
    r = np.zeros((4, 16), np.float32)
    for g, w in enumerate((2, 4, 8, 16)):
        for t in range(16):
            r[g, t] = 1.0 / min(w, t + 1)
    rcnt = np.broadcast_to(r.reshape(1, 64), (128, 64)).copy()
    return ident, rcnt


def core_inputs(c, x_prompt, x_sample, cache_pool, c_prompt, c_sample, g_norm, w_ada, b_ada, sgu_w_in, sgu_g_v,
                sgu_w_s, sgu_b_s, sgu_w_out, pool_w_grp, pool_scale, ffn_w_up, ffn_w_down, shared):
    f = np.ascontiguousarray
    rows0 = np.concatenate([g_norm.reshape(16, D), c_prompt[c:c + 1], c_sample[2 * c:2 * c + 2], pool_scale.reshape(2, D)], axis=0)
    m = dict(shared)
    m.update({
        "xp": f(x_prompt[c]),
        "xs": f(x_sample[2 * c:2 * c + 2].reshape(64, D)),
        "cache": f(cache_pool[:, 2 * c:2 * c + 2].reshape(2, 30, D)),
        "rows0": f(rows0.astype(np.float32)),
    })
    return m


_CACHE = {}


def kernel(x_prompt, x_sample, cache_pool, c_prompt, c_sample, g_norm, w_ada, b_ada, sgu_w_in, sgu_g_v,
           sgu_w_s, sgu_b_s, sgu_w_out, pool_w_grp, pool_scale, ffn_w_up, ffn_w_down):
    args = [np.asarray(a, dtype=np.float32) for a in (x_prompt, x_sample, cache_pool, c_prompt, c_sample, g_norm, w_ada, b_ada,
                                                      sgu_w_in, sgu_g_v, sgu_w_s, sgu_b_s, sgu_w_out, pool_w_grp, pool_scale,
                                                      ffn_w_up, ffn_w_down)]
    (x_prompt, x_sample, cache_pool, c_prompt, c_sample, g_norm, w_ada, b_ada, sgu_w_in, sgu_g_v, sgu_w_s, sgu_b_s,
     sgu_w_out, pool_w_grp, pool_scale, ffn_w_up, ffn_w_down) = args
    B, SEQ, _ = x_prompt.shape
    n_cores = B
    ident, rcnt = make_consts()
    f = np.ascontiguousarray
    shared = {"w_ada": f(w_ada), "b_ada": f(b_ada), "w_in": f(sgu_w_in), "g_v": f(sgu_g_v), "w_s": f(sgu_w_s),
              "b_s": f(sgu_b_s.reshape(1, 2048)), "w_out": f(sgu_w_out), "pw": f(pool_w_grp), "w_up": f(ffn_w_up),
              "w_dn": f(ffn_w_down), "ident": ident, "rcnt": rcnt}
    in_maps = [core_inputs(c, *args, shared) for c in range(n_cores)]
    if SEQ not in _CACHE:
        _CACHE[SEQ] = build_program(SEQ)[0]
    nc = _CACHE[SEQ]
    res = run_bass_kernel_spmd(nc, in_maps, core_ids=list(range(n_cores)))
    R = res.results
    y_prompt = np.stack([R[c]["yp"] for c in range(n_cores)], axis=0)
    y_sample = np.concatenate([R[c]["ys"].reshape(2, 32, D) for c in range(n_cores)], axis=0)
    spp = np.stack([R[c]["spp"] for c in range(n_cores)], axis=1)
    ssg = np.concatenate([R[c]["ssg"].reshape(2, 2, 32, 2048) for c in range(n_cores)], axis=1)
    sps = np.concatenate([R[c]["sps"].reshape(2, 2, 15, D) for c in range(n_cores)], axis=1)
    return (y_prompt.astype(np.float32), y_sample.astype(np.float32), spp.astype(np.float32),
            ssg.astype(np.float32), sps.astype(np.float32))
```
